# Optimizing a Trainium2 kernel written in Bass

```python
import math
import jax, jax.numpy as jnp
from jax import lax
import numpy as np

D_MODEL = 2048
BATCH = 32
SEQ = 256
DEPTH = 4
DEC_BATCH = 2
DEC_SEQ = 1024
PAST_LEN = 512

GRID_W = 64
N_EVEN = (DEPTH + 1) // 2
N_ODD = DEPTH // 2
EPS = 1e-6
ROPE_BASE = 10000.0
Q_BLOCK = 128
A_HEADS = 8
A_DK = 64
A_DV = 2 * A_DK
A_QK = A_HEADS * 2 * A_DK
A_WIDTH = A_HEADS * A_DV
B_WIDTH = D_MODEL // 2
B_KERNEL = 31
B_PAD = (B_KERNEL - 1) // 2
C_WIDTH = D_MODEL // 2
C_GROUP = 16
C_GROUPS = C_WIDTH // C_GROUP
C_STATE = 64
D_HEADS = 8
D_KV_HEADS = 2
D_REP = D_HEADS // D_KV_HEADS
D_HEAD_DIM = 128
D_WIDTH = D_HEADS * D_HEAD_DIM
D_KV_WIDTH = D_KV_HEADS * D_HEAD_DIM
EVEN_IN = 2 * A_QK + A_WIDTH + 2 * B_WIDTH
ODD_IN = C_WIDTH + D_WIDTH + 2 * D_KV_WIDTH
D_FF = 4 * D_MODEL
N_MOD = 6

kernel_name = "hybrid_diffusion_prefix_trunk_step"

F32 = jnp.float32


def rms_norm(x, g):
    xf = x.astype(F32)
    y = xf * lax.rsqrt(jnp.mean(xf * xf, axis=-1, keepdims=True) + EPS)
    return (y * g.astype(F32)).astype(x.dtype)


def layer_norm(x, g, b):
    xf = x.astype(F32)
    mu = jnp.mean(xf, axis=-1, keepdims=True)
    xc = xf - mu
    var = jnp.mean(xc * xc, axis=-1, keepdims=True)
    return (xc * lax.rsqrt(var + EPS) * g.astype(F32) + b.astype(F32)).astype(x.dtype)


def ada_modulation(cond, w, bias):
    m = jax.nn.silu(cond.astype(F32)).astype(w.dtype) @ w + bias
    return jnp.split(m[:, None, :], N_MOD, axis=-1)


def modulate(x, g, shift, scale):
    return rms_norm(x, g) * (1.0 + scale) + shift


def axial_rope(n_tokens, dim):
    rows = n_tokens // GRID_W
    row = jnp.repeat(jnp.arange(rows, dtype=F32), GRID_W)
    col = jnp.tile(jnp.arange(GRID_W, dtype=F32), rows)
    quarter = dim // 4
    inv_freq = ROPE_BASE ** (-jnp.arange(quarter, dtype=F32) / quarter)
    ang_r = row[:, None] * inv_freq[None, :]
    ang_c = col[:, None] * inv_freq[None, :]
    ang = jnp.concatenate([ang_r, ang_r, ang_c, ang_c], axis=-1)
    return jnp.cos(ang), jnp.sin(ang)


def apply_rope(x, cos, sin):
    xf = x.astype(F32)
    x1, x2, x3, x4 = jnp.split(xf, 4, axis=-1)
    rot = jnp.concatenate([-x2, x1, -x4, x3], axis=-1)
    return (xf * cos + rot * sin).astype(x.dtype)


def sweep_queries(block_fn, q):
    *lead, t, d = q.shape
    nb = t // Q_BLOCK
    qb = jnp.moveaxis(q.reshape(*lead, nb, Q_BLOCK, d), -3, 0)
    out = jnp.moveaxis(lax.map(block_fn, qb), 0, -3)
    return out.reshape(*lead, t, out.shape[-1])


def diff_attention(q, k, v, lam):
    k1, k2 = k[..., :A_DK], k[..., A_DK:]
    scale = A_DK ** -0.5

    def block(qb):
        q1, q2 = qb[..., :A_DK], qb[..., A_DK:]
        p1 = jax.nn.softmax(jnp.einsum('bhqd,bhkd->bhqk', q1, k1).astype(F32) * scale, axis=-1)
        p2 = jax.nn.softmax(jnp.einsum('bhqd,bhkd->bhqk', q2, k2).astype(F32) * scale, axis=-1)
        p = (p1 - lam * p2).astype(v.dtype)
        return jnp.einsum('bhqk,bhkd->bhqd', p, v)

    return sweep_queries(block, q)


def gqa_attention(q, k, v):
    scale = D_HEAD_DIM ** -0.5

    def block(qb):
        s = jnp.einsum('bgrqd,bgkd->bgrqk', qb, k).astype(F32) * scale
        p = jax.nn.softmax(s, axis=-1).astype(v.dtype)
        return jnp.einsum('bgrqk,bgkd->bgrqd', p, v)

    return sweep_queries(block, q)


def conformer_conv(z, w, bias, ln):
    a, g = jnp.split(z, 2, axis=-1)
    h = a * jax.nn.sigmoid(g)
    h = lax.conv_general_dilated(h, w[:, None, :], window_strides=(1,), padding=[(B_PAD, B_PAD)],
                                 dimension_numbers=('NWC', 'WIO', 'NWC'),
                                 feature_group_count=B_WIDTH) + bias
    h = layer_norm(h, ln[0], ln[1])
    return jax.nn.silu(h)


def s5_discretize(a_re, a_im, log_dt, b_re, b_im):
    dt = jnp.exp(log_dt.astype(F32))[:, None]
    ar = a_re.astype(F32)
    ai = a_im.astype(F32)
    mag = jnp.exp(ar * dt)
    abar_re = mag * jnp.cos(ai * dt)
    abar_im = mag * jnp.sin(ai * dt)
    den = ar * ar + ai * ai
    nr = abar_re - 1.0
    ni = abar_im
    k_re = ((nr * ar + ni * ai) / den)[..., None]
    k_im = ((ni * ar - nr * ai) / den)[..., None]
    br = b_re.astype(F32)
    bi = b_im.astype(F32)
    return abar_re, abar_im, k_re * br - k_im * bi, k_re * bi + k_im * br


def complex_affine_combine(e1, e2):
    a1r, a1i, b1r, b1i = e1
    a2r, a2i, b2r, b2i = e2
    return (a2r * a1r - a2i * a1i,
            a2r * a1i + a2i * a1r,
            a2r * b1r - a2i * b1i + b2r,
            a2r * b1i + a2i * b1r + b2i)


def s5_direction(u, abar_re, abar_im, bbar_re, bbar_im, c_re, c_im, h0_re, h0_im, reverse):
    bu_re = jnp.einsum('btgh,gph->btgp', u, bbar_re)
    bu_im = jnp.einsum('btgh,gph->btgp', u, bbar_im)
    first = -1 if reverse else 0
    bu_re = bu_re.at[:, first].add(abar_re * h0_re - abar_im * h0_im)
    bu_im = bu_im.at[:, first].add(abar_re * h0_im + abar_im * h0_re)
    a_re = jnp.broadcast_to(abar_re, bu_re.shape)
    a_im = jnp.broadcast_to(abar_im, bu_im.shape)
    _, _, h_re, h_im = lax.associative_scan(complex_affine_combine, (a_re, a_im, bu_re, bu_im),
                                            reverse=reverse, axis=1)
    y = (jnp.einsum('gnp,btgp->btgn', c_re.astype(F32), h_re)
         - jnp.einsum('gnp,btgp->btgn', c_im.astype(F32), h_im))
    last = 0 if reverse else -1
    return y, h_re[:, last], h_im[:, last]


def s5_mixer(u, a_re, a_im, log_dt, b, c, d, glu_w, glu_b, h0):
    bsz, t, _ = u.shape
    uf = u.astype(F32).reshape(bsz, t, C_GROUPS, C_GROUP)
    h0 = h0.astype(F32)
    y = d.astype(F32).reshape(C_GROUPS, C_GROUP) * uf
    finals = []
    for direction, rev in enumerate((False, True)):
        abr, abi, bbr, bbi = s5_discretize(a_re[direction], a_im[direction], log_dt[direction],
                                           b[direction, 0], b[direction, 1])
        yd, hr, hi = s5_direction(uf, abr, abi, bbr, bbi, c[direction, 0], c[direction, 1],
                                  h0[:, direction, 0], h0[:, direction, 1], rev)
        y = y + yd
        finals.append(jnp.stack([hr, hi], axis=1))
    state = jnp.stack(finals, axis=1)
    y = jax.nn.gelu(y.reshape(bsz, t, C_WIDTH)).astype(u.dtype)
    return y * jax.nn.sigmoid(y @ glu_w + glu_b), state


def even_mixer(h, lam_init, w_in, w_out, lam_p, subln, conv_w, conv_b, conv_ln, ctx_k, ctx_v, rope):
    bsz, t, _ = h.shape
    proj = h @ w_in
    q, k, v, z = jnp.split(proj, [A_QK, 2 * A_QK, 2 * A_QK + A_WIDTH], axis=-1)
    q = q.reshape(bsz, t, A_HEADS, 2 * A_DK).transpose(0, 2, 1, 3)
    k = k.reshape(bsz, t, A_HEADS, 2 * A_DK).transpose(0, 2, 1, 3)
    v = v.reshape(bsz, t, A_HEADS, A_DV).transpose(0, 2, 1, 3)
    if rope is not None:
        cos, sin = rope
        cos, sin = cos[:, None, :], sin[:, None, :]
        q = apply_rope(q.reshape(bsz, A_HEADS, t, 2, A_DK), cos, sin).reshape(bsz, A_HEADS, t, 2 * A_DK)
        k = apply_rope(k.reshape(bsz, A_HEADS, t, 2, A_DK), cos, sin).reshape(bsz, A_HEADS, t, 2 * A_DK)
    keys = k if ctx_k is None else jnp.concatenate([k, ctx_k.astype(k.dtype)], axis=2)
    vals = v if ctx_v is None else jnp.concatenate([v, ctx_v.astype(v.dtype)], axis=2)
    lp = lam_p.astype(F32)
    lam = jnp.exp(jnp.sum(lp[0] * lp[1])) - jnp.exp(jnp.sum(lp[2] * lp[3])) + lam_init
    att = diff_attention(q, keys, vals, lam)
    att = rms_norm(att, subln) * (1.0 - lam_init)
    att = att.transpose(0, 2, 1, 3).reshape(bsz, t, A_WIDTH)
    conv = conformer_conv(z, conv_w, conv_b, conv_ln).astype(att.dtype)
    out = jnp.concatenate([att, conv], axis=-1) @ w_out
    return out, k, v


def odd_mixer(h, w_in, w_out, a_re, a_im, log_dt, ssm_b, ssm_c, ssm_d, glu_w, glu_b, qk_g,
              ctx_k, ctx_v, h0, rope):
    bsz, t, _ = h.shape
    proj = h @ w_in
    u, q, k, v = jnp.split(proj, [C_WIDTH, C_WIDTH + D_WIDTH, C_WIDTH + D_WIDTH + D_KV_WIDTH], axis=-1)
    ssm_out, state = s5_mixer(u, a_re, a_im, log_dt, ssm_b, ssm_c, ssm_d, glu_w, glu_b, h0)
    q = rms_norm(q.reshape(bsz, t, D_HEADS, D_HEAD_DIM), qk_g[0])
    k = rms_norm(k.reshape(bsz, t, D_KV_HEADS, D_HEAD_DIM), qk_g[1])
    v = v.reshape(bsz, t, D_KV_HEADS, D_HEAD_DIM)
    if rope is not None:
        cos, sin = rope
        q = apply_rope(q, cos[:, None, :], sin[:, None, :])
        k = apply_rope(k, cos[:, None, :], sin[:, None, :])
    q = q.reshape(bsz, t, D_KV_HEADS, D_REP, D_HEAD_DIM).transpose(0, 2, 3, 1, 4)
    k = k.transpose(0, 2, 1, 3)
    v = v.transpose(0, 2, 1, 3)
    keys = k if ctx_k is None else jnp.concatenate([k, ctx_k.astype(k.dtype)], axis=2)
    vals = v if ctx_v is None else jnp.concatenate([v, ctx_v.astype(v.dtype)], axis=2)
    att = gqa_attention(q, keys, vals)
    att = att.transpose(0, 3, 1, 2, 4).reshape(bsz, t, D_WIDTH)
    out = jnp.concatenate([ssm_out.astype(att.dtype), att], axis=-1) @ w_out
    return out, k, v, state


def sq_relu_mlp(h, w1, w2):
    a = jax.nn.relu(h @ w1)
    return (a * a) @ w2


def setup_inputs(seed: int = 0) -> dict:
    key = jax.random.key(seed)
    ks = iter(jax.random.split(key, 48))

    def nrm(shape, scale):
        return jax.random.normal(next(ks), shape, F32) * scale

    return {
        'x_prompt': nrm((BATCH, SEQ, D_MODEL), 1.0),
        'x_sample': nrm((DEC_BATCH, DEC_SEQ, D_MODEL), 1.0),
        'c': nrm((DEC_BATCH, D_MODEL), 1.0),
        'cache_a_k': nrm((DEC_BATCH, N_EVEN, A_HEADS, PAST_LEN, 2 * A_DK), 1.0),
        'cache_a_v': nrm((DEC_BATCH, N_EVEN, A_HEADS, PAST_LEN, A_DV), 1.0),
        'cache_d_k': nrm((DEC_BATCH, N_ODD, D_KV_HEADS, PAST_LEN, D_HEAD_DIM), 1.0),
        'cache_d_v': nrm((DEC_BATCH, N_ODD, D_KV_HEADS, PAST_LEN, D_HEAD_DIM), 1.0),
        'state_c_ssm': nrm((DEC_BATCH, N_ODD, 2, 2, C_GROUPS, C_STATE), 0.5),
        'c_ctx': nrm((D_MODEL,), 1.0),
        'ada_w': nrm((DEPTH, D_MODEL, N_MOD * D_MODEL), 0.5 * D_MODEL ** -0.5),
        'ada_b': nrm((DEPTH, N_MOD * D_MODEL), 0.02),
        'norm_g': 1.0 + nrm((DEPTH, 2, D_MODEL), 0.02),
        'mlp_w1': nrm((DEPTH, D_MODEL, D_FF), D_MODEL ** -0.5),
        'mlp_w2': nrm((DEPTH, D_FF, D_MODEL), D_FF ** -0.5),
        'even_w_in': nrm((N_EVEN, D_MODEL, EVEN_IN), D_MODEL ** -0.5),
        'even_w_out': nrm((N_EVEN, A_WIDTH + B_WIDTH, D_MODEL), (A_WIDTH + B_WIDTH) ** -0.5),
        'diff_lambda': nrm((N_EVEN, 4, A_DK), 0.1),
        'diff_subln': 1.0 + nrm((N_EVEN, A_DV), 0.02),
        'conv_w': nrm((N_EVEN, B_KERNEL, B_WIDTH), B_KERNEL ** -0.5),
        'conv_b': nrm((N_EVEN, B_WIDTH), 0.02),
        'conv_ln': jnp.stack([1.0 + nrm((N_EVEN, B_WIDTH), 0.02), nrm((N_EVEN, B_WIDTH), 0.02)], axis=1),
        'odd_w_in': nrm((N_ODD, D_MODEL, ODD_IN), D_MODEL ** -0.5),
        'odd_w_out': nrm((N_ODD, C_WIDTH + D_WIDTH, D_MODEL), (C_WIDTH + D_WIDTH) ** -0.5),
        'ssm_a_re': -0.5 + nrm((N_ODD, 2, C_GROUPS, C_STATE), 0.01),
        'ssm_a_im': jnp.pi * jnp.arange(C_STATE, dtype=F32) + nrm((N_ODD, 2, C_GROUPS, C_STATE), 0.01),
        'ssm_log_dt': jax.random.uniform(next(ks), (N_ODD, 2, C_GROUPS), F32,
                                         minval=math.log(1e-3), maxval=math.log(1e-1)),
        'ssm_b': nrm((N_ODD, 2, 2, C_GROUPS, C_STATE, C_GROUP), (2 * C_GROUP) ** -0.5),
        'ssm_c': nrm((N_ODD, 2, 2, C_GROUPS, C_GROUP, C_STATE), C_STATE ** -0.5),
        'ssm_d': nrm((N_ODD, C_WIDTH), 0.5),
        'ssm_glu_w': nrm((N_ODD, C_WIDTH, C_WIDTH), C_WIDTH ** -0.5),
        'ssm_glu_b': nrm((N_ODD, C_WIDTH), 0.02),
        'qk_norm': 1.0 + nrm((N_ODD, 2, D_HEAD_DIM), 0.02),
        'final_norm': 1.0 + nrm((D_MODEL,), 0.02),
    }


def reference(x_prompt, x_sample, c, cache_a_k, cache_a_v, cache_d_k, cache_d_v, state_c_ssm,
              c_ctx, ada_w, ada_b, norm_g, mlp_w1, mlp_w2, even_w_in, even_w_out, diff_lambda,
              diff_subln, conv_w, conv_b, conv_ln, odd_w_in, odd_w_out, ssm_a_re, ssm_a_im,
              ssm_log_dt, ssm_b, ssm_c, ssm_d, ssm_glu_w, ssm_glu_b, qk_norm, final_norm):
    n_lat = x_sample.shape[1]
    rope_a = axial_rope(n_lat, A_DK)
    rope_d = axial_rope(n_lat, D_HEAD_DIM)
    bp = x_prompt.shape[0]
    xp, xs = x_prompt, x_sample
    a_k, a_v, d_k, d_v, c_st = [], [], [], [], []
    for l in range(DEPTH):
        sp1, cp1, gp1, sp2, cp2, gp2 = ada_modulation(c_ctx[None, :], ada_w[l], ada_b[l])
        ss1, cs1, gs1, ss2, cs2, gs2 = ada_modulation(c, ada_w[l], ada_b[l])
        hp = modulate(xp, norm_g[l, 0], sp1, cp1)
        hs = modulate(xs, norm_g[l, 0], ss1, cs1)
        if l % 2 == 0:
            e = l // 2
            lam_init = 0.8 - 0.6 * math.exp(-0.3 * l)
            mp, kp, vp = even_mixer(hp, lam_init, even_w_in[e], even_w_out[e], diff_lambda[e],
                                    diff_subln[e], conv_w[e], conv_b[e], conv_ln[e], None, None, None)
            ms, _, _ = even_mixer(hs, lam_init, even_w_in[e], even_w_out[e], diff_lambda[e],
                                  diff_subln[e], conv_w[e], conv_b[e], conv_ln[e],
                                  cache_a_k[:, e], cache_a_v[:, e], rope_a)
            a_k.append(kp)
            a_v.append(vp)
        else:
            o = l // 2
            h0p = jnp.zeros((bp, 2, 2, C_GROUPS, C_STATE), F32)
            mp, kp, vp, stp = odd_mixer(hp, odd_w_in[o], odd_w_out[o], ssm_a_re[o], ssm_a_im[o],
                                        ssm_log_dt[o], ssm_b[o], ssm_c[o], ssm_d[o], ssm_glu_w[o],
                                        ssm_glu_b[o], qk_norm[o], None, None, h0p, None)
            ms, _, _, _ = odd_mixer(hs, odd_w_in[o], odd_w_out[o], ssm_a_re[o], ssm_a_im[o],
                                    ssm_log_dt[o], ssm_b[o], ssm_c[o], ssm_d[o], ssm_glu_w[o],
                                    ssm_glu_b[o], qk_norm[o], cache_d_k[:, o], cache_d_v[:, o],
                                    state_c_ssm[:, o], rope_d)
            d_k.append(kp)
            d_v.append(vp)
            c_st.append(stp)
        xp = xp + gp1 * mp
        xs = xs + gs1 * ms
        xp = xp + gp2 * sq_relu_mlp(modulate(xp, norm_g[l, 1], sp2, cp2), mlp_w1[l], mlp_w2[l])
        xs = xs + gs2 * sq_relu_mlp(modulate(xs, norm_g[l, 1], ss2, cs2), mlp_w1[l], mlp_w2[l])
    y_prompt = rms_norm(xp, final_norm)
    y_sample = rms_norm(xs, final_norm)
    new_a_k = jnp.stack(a_k, axis=1)
    new_a_v = jnp.stack(a_v, axis=1)
    new_d_k = jnp.stack(d_k, axis=1)
    new_d_v = jnp.stack(d_v, axis=1)
    new_c_ssm = jnp.stack(c_st, axis=1)
    return (y_prompt, y_sample, new_a_k, new_a_v, new_d_k, new_d_v, new_c_ssm)
```

```python
import math
import os
from contextlib import ExitStack
import numpy as np
import concourse.bass as bass
import concourse.mybir as mybir
from concourse.bass_utils import run_bass_kernel_spmd

F32 = mybir.dt.float32
BF16 = mybir.dt.bfloat16
I32 = mybir.dt.int32
AF = mybir.ActivationFunctionType
ALU = mybir.AluOpType
AX = mybir.AxisListType

N_DMA_SEMS = 40
EPS = 1e-6
TT = 1024
NWR = 3


class Buf:
    __slots__ = ("ap", "keys")

    def __init__(self, ap, keys):
        self.ap = ap
        self.keys = keys


def _flat(items):
    out = []
    for it in items:
        if isinstance(it, Buf):
            out.extend(it.keys)
        elif isinstance(it, list):
            out.extend(_flat(it))
        else:
            out.append(it)
    return out


class Prog:
    def __init__(self, nc):
        self.nc = nc
        self.ops = []
        self.last_w = {}
        self.readers = {}

    def add(self, eng, fn, r=(), w=(), dma=False):
        reads = _flat(r)
        writes = _flat(w)
        idx = len(self.ops)
        deps = set()
        lw, rd = self.last_w, self.readers
        for k in reads:
            x = lw.get(k)
            if x is not None:
                deps.add(x)
        for k in writes:
            x = lw.get(k)
            if x is not None:
                deps.add(x)
            rr = rd.get(k)
            if rr:
                deps.update(rr)
        for k in reads:
            rd.setdefault(k, []).append(idx)
        for k in writes:
            lw[k] = idx
            rd[k] = []
        self.ops.append([eng, fn, deps, dma, False, None])
        return idx

    def pe(self, fn, r=(), w=()):
        return self.add('pe', fn, r, w)

    def act(self, fn, r=(), w=()):
        return self.add('act', fn, r, w)

    def dve(self, fn, r=(), w=()):
        return self.add('dve', fn, r, w)

    def dma(self, q, fn, r=(), w=()):
        return self.add(q, fn, r, w, dma=True)

    def emit(self):
        nc = self.nc
        ops = self.ops
        for op in ops:
            best = {}
            keep = set()
            for d in op[2]:
                dop = ops[d]
                if dop[3]:
                    keep.add(d)
                elif best.get(dop[0], -1) < d:
                    best[dop[0]] = d
            keep.update(best.values())
            op[2] = keep
        for op in ops:
            eng, fn, deps, dma, _, _ = op
            for d in deps:
                dop = ops[d]
                if dop[0] == 'pe' and eng == 'pe' and not dop[3] and not dma:
                    continue
                dop[4] = True
        with ExitStack() as st:
            cnt = {e: 0 for e in ('pe', 'act', 'dve', 'pool')}
            for op in ops:
                if (not op[3]) and op[4]:
                    cnt[op[0]] += 1
            CH = {e: max(3000, -(-cnt[e] // 13)) for e in cnt}
            csem = {e: [st.enter_context(nc.semaphore('s_%s_%d' % (e, k))) for k in range(max(1, -(-cnt[e] // CH[e])))]
                    for e in cnt}
            dsems = [st.enter_context(nc.semaphore('d%d' % i)) for i in range(N_DMA_SEMS)]
            print("signal counts", cnt, "epoch sizes", CH, "n_ops", len(ops))
            ccount = {e: 0 for e in cnt}
            dcount = [0] * N_DMA_SEMS
            dnext = 0
            for op in ops:
                eng, fn, deps, dma, sig, _ = op
                if dma:
                    s = dnext
                    dnext = (dnext + 1) % N_DMA_SEMS
                    prev = dcount[s]
                    dcount[s] += 16
                    op[5] = ('d', s, dcount[s], prev)
                elif sig:
                    n = ccount[eng]
                    ccount[eng] += 1
                    op[5] = ('c', eng, n // CH[eng], n % CH[eng] + 1)
            streams = {e: [] for e in ('pe', 'act', 'dve', 'pool', 'sp')}
            for i, op in enumerate(ops):
                streams[op[0]].append(i)
            block = st.enter_context(nc.Block())

            def run_stream(ename, eng):
                waited = {}
                waited_c = {}

                def wait(sem, key, val):
                    if waited.get(key, 0) >= val:
                        return
                    waited[key] = val
                    eng.wait_ge(sem, val)

                def wait_c(pe_, ep, val):
                    cur = waited_c.get(pe_, (-1, 0))
                    if (ep, val) <= cur:
                        return
                    waited_c[pe_] = (ep, val)
                    eng.wait_ge(csem[pe_][ep], val)

                for i in streams[ename]:
                    e, fn, deps, dma, sig, tok = ops[i]
                    for d in sorted(deps):
                        t = ops[d][5]
                        if t is None:
                            continue
                        if t[0] == 'c':
                            if t[1] == 'pe' and ename == 'pe' and not dma:
                                continue
                            wait_c(t[1], t[2], t[3])
                        else:
                            wait(dsems[t[1]], ('d', t[1]), t[2])
                    if dma:
                        _, s, val, prev = tok
                        if prev:
                            wait(dsems[s], ('d', s), prev)
                        fn(eng).then_inc(dsems[s], 16)
                    else:
                        ins = fn(eng)
                        if sig:
                            ins.then_inc(csem[e][tok[2]], 1)
                if ename == 'sp':
                    for s in range(N_DMA_SEMS):
                        if dcount[s]:
                            wait(dsems[s], ('d', s), dcount[s])

            @block.tensor
            def _(eng):
                run_stream('pe', eng)

            @block.scalar
            def _(eng):
                run_stream('act', eng)

            @block.vector
            def _(eng):
                run_stream('dve', eng)

            @block.gpsimd
            def _(eng):
                run_stream('pool', eng)

            @block.sync
            def _(eng):
                run_stream('sp', eng)


IN_SPECS = [
    ("xg", [2, 128, 16, TT]), ("cond", [128, 16, 2]),
    ("ada_w", [4, 2048, 12288]), ("ada_bT", [128, 4, 96]),
    ("norm_gT", [128, 4, 2, 16]), ("final_normT", [128, 16]),
    ("mlp_w1", [4, 2048, 8192]), ("mlp_w2", [4, 8192, 2048]),
    ("even_w_in", [2, 2048, 5120]), ("even_w_out", [2, 2048, 2048]),
    ("odd_w_in", [2, 2048, 2560]), ("odd_w_out", [2, 2048, 2048]),
    ("glu_w", [2, 1024, 1024]),
    ("dlam", [128, 2, 4, 64]), ("sublnT", [128, 2]),
    ("conv_wT", [128, 2, 8, 31]), ("conv_bT", [128, 2, 8]), ("conv_lnT", [128, 2, 2, 8]),
    ("a_reT", [128, 2, 64]), ("a_imT", [128, 2, 64]), ("ldtT", [128, 2, 64]),
    ("ssm_bz", [2, 2, 32, 128, 2, 128]), ("ssm_cz", [2, 2, 32, 128, 2, 128]),
    ("ssm_dT", [128, 2, 8]), ("glu_bT", [128, 2, 8]), ("qk_normT", [128, 2, 2]),
    ("h0T", [128, 2, 2, 64]),
    ("ca_kT", [2, 8, 128, 512]), ("ca_v", [2, 8, 512, 128]),
    ("cd_kT", [2, 2, 128, 512]), ("cd_v", [2, 2, 512, 128]),
    ("ident", [128, 128]), ("ropeA", [2, 128, TT]), ("ropeD", [2, 128, TT]),
    ("permA", [128, 128]), ("permD", [128, 128]), ("tvec", [128, TT]),
]
OUT_SPECS = [
    ("yg", [2, 128, 16, TT]),
    ("o_akT", [2, 8, 128, TT]), ("o_av", [2, TT, 1024]),
    ("o_dkT", [2, 2, 128, TT]), ("o_dv", [2, TT, 256]),
    ("o_ssm", [2, 128, 2, 2, 32, 4]),
] + ([("dbg", [6, 128, 1024])] if os.environ.get("MK_DBG") else [])


def build():
    nc = bass.Bass("TRN2", target_bir_lowering=False)
    D = {}
    for name, shp in IN_SPECS:
        D[name] = nc.dram_tensor(name, list(shp), F32, kind="ExternalInput").ap()
    for name, shp in OUT_SPECS:
        D[name] = nc.dram_tensor(name, list(shp), F32, kind="ExternalOutput").ap()
    P = Prog(nc)
    with ExitStack() as st:
        def sb(name, shape, dt=F32):
            return st.enter_context(nc.sbuf_tensor("sb_" + name, list(shape), dt))

        X = sb("X", [128, 16, TT])
        H = sb("H", [128, 16, TT], BF16)
        MIX = sb("MIX", [128, 16, TT], BF16)
        WR = sb("WR", [128, NWR, 4096], BF16)
        SCRW = 9216
        SCR = sb("SCR", [128, SCRW])
        ps = [st.enter_context(nc.psum_tensor("ps%d" % i, [128, 512], F32)) for i in range(8)]
        PSK = [('ps', i) for i in range(8)]

        ident = sb("ident", [128, 128]); ones_f = sb("ones_f", [128, 128]); ones_b = sb("ones_b", [128, 128], BF16)
        permA_f = sb("permA_f", [128, 128]); permD_f = sb("permD_f", [128, 128])
        permA = sb("permA", [128, 128], BF16); permD = sb("permD", [128, 128], BF16)
        MOD = sb("MOD", [128, 4, 96, 2]); ADAB = sb("ADAB", [128, 4, 96]); NORMG = sb("NORMG", [128, 4, 2, 16])
        FNORM = sb("FNORM", [128, 16]); CONDF = sb("CONDF", [128, 16, 2]); SC = sb("SC", [128, 16, 2], BF16)
        GS = sb("GS", [128, 16])
        DLAM = sb("DLAM", [128, 2, 4, 64]); SUBLN = sb("SUBLN", [128, 2]); LT = sb("LT", [128, 64]); LS = sb("LS", [128, 4])
        CONVW = sb("CONVW", [128, 2, 8, 31]); CONVB = sb("CONVB", [128, 2, 8]); CONVLN = sb("CONVLN", [128, 2, 2, 8])
        SSMD = sb("SSMD", [128, 2, 8]); GLUB = sb("GLUB", [128, 2, 8]); QKN = sb("QKN", [128, 2, 2])
        ARE = sb("ARE", [128, 2, 64]); AIM = sb("AIM", [128, 2, 64]); LDT = sb("LDT", [128, 2, 64]); H0 = sb("H0", [128, 2, 2, 64])
        SP = sb("SP", [128, 12, 64]); SPI = sb("SPI", [128, 64], I32)

        def cload(t, src, key):
            P.dma('sp', lambda e: e.dma_start(out=t[:], in_=src), w=[key])

        for t, n in [(ident, "ident"), (permA_f, "permA"), (permD_f, "permD"), (ADAB, "ada_bT"), (NORMG, "norm_gT"),
                     (FNORM, "final_normT"), (CONDF, "cond"), (DLAM, "dlam"), (SUBLN, "sublnT"), (CONVW, "conv_wT"),
                     (CONVB, "conv_bT"), (CONVLN, "conv_lnT"), (SSMD, "ssm_dT"), (GLUB, "glu_bT"), (QKN, "qk_normT"),
                     (ARE, "a_reT"), (AIM, "a_imT"), (LDT, "ldtT"), (H0, "h0T")]:
            cload(t, D[n], 'c_' + n)
        CK = ['c_' + n for n in ["ident", "permA", "permD", "ada_bT", "norm_gT", "final_normT", "cond", "dlam", "sublnT",
                                 "conv_wT", "conv_bT", "conv_lnT", "ssm_dT", "glu_bT", "qk_normT", "a_reT", "a_imT", "ldtT", "h0T"]]
        P.dve(lambda e: e.memset(ones_f[:], 1.0), w=['ones'])
        P.dve(lambda e: e.memset(ones_b[:], 1.0), w=['ones'])
        P.dve(lambda e: e.tensor_copy(out=permA[:], in_=permA_f[:]), r=['c_permA'], w=['perm'])
        P.dve(lambda e: e.tensor_copy(out=permD[:], in_=permD_f[:]), r=['c_permD'], w=['perm'])
        CONST = ['ones', 'perm'] + CK
        if SKIP - {''}:
            P.dve(lambda e: e.memset(MIX[:], 0.0), w=[('MIX', c) for c in range(16)])

        def scr(off, n, dt=F32):
            ap = SCR[:, off:off + n]
            if dt != F32:
                ap = ap.bitcast(dt)
            return Buf(ap, [('scr', g) for g in range(off // 256, (off + n + 255) // 256)])

        MIXflat = MIX[:].rearrange("p a b -> p (a b)").bitcast(F32)

        def mixf(off, n, dt=F32):
            ap = MIXflat[:, off:off + n]
            if dt != F32:
                ap = ap.bitcast(dt)
            return Buf(ap, [('MIX', c) for c in range(off // 512, (off + n + 511) // 512)])

        def sub(b, ap):
            return Buf(ap, b.keys)

        wr_n = [0]

        def wload(pieces):
            s = wr_n[0] % NWR
            wr_n[0] += 1
            n = len(pieces)
            for i, (dst_fn, src) in enumerate(pieces):
                keys = [('wr', s, i)] if n == 2 else [('wr', s, 0), ('wr', s, 1)]
                P.dma('pool', lambda e, d=dst_fn(s), src=src: e.dma_start(out=d, in_=src), w=keys)
            return s

        def wrk(s):
            return [('wr', s, 0), ('wr', s, 1)]

        def wview(s, k, c):
            return WR[:, s, 0:k * c].rearrange("p (k c) -> p k c", k=k)

        def hv_sl(hv):
            return slice(hv * 512, (hv + 1) * 512)

        bank_rr = [0]

        def nbank(lo=0, n=8):
            b = lo + bank_rr[0] % n
            bank_rr[0] += 1
            return b

        P.act(lambda e: e.activation(out=SC[:], in_=CONDF[:], func=AF.Silu), r=['c_cond'], w=['SC'])
        for l in range(4):
            wv = D["ada_w"][l].rearrange("(kc p) n -> p kc n", p=128)
            pb = l % 2
            for blk in range(48):
                s = wload([(lambda s: wview(s, 16, 256), wv[:, :, blk * 256:(blk + 1) * 256])])
                for mi in range(2):
                    mc = blk * 2 + mi
                    for kc in range(16):
                        P.pe(lambda e, s=s, mi=mi, mc=mc, kc=kc, pb=pb: e.matmul(
                            ps[pb][:, mc * 2:mc * 2 + 2], lhsT=wview(s, 16, 256)[:, kc, mi * 128:(mi + 1) * 128],
                            rhs=SC[:, kc, :], start=(kc == 0), stop=(kc == 15)),
                            r=wrk(s) + ['SC'], w=[PSK[pb]])
            for cnd in range(2):
                P.dve(lambda e, l=l, cnd=cnd, pb=pb: e.tensor_add(
                    out=MOD[:, l, :, cnd], in0=ps[pb][:, 0:192].rearrange("p (m c) -> p m c", c=2)[:, :, cnd],
                    in1=ADAB[:, l, :]), r=[PSK[pb], 'c_ada_bT'], w=['MOD'])

        def XK(kc):
            return ('X', kc)

        def HK(kc):
            return ('H', kc)

        ALLH = [HK(k) for k in range(16)]

        def phase_norm(l, which, cnd, final=False, grp=0):
            SQ = [scr(0, 1024), scr(1024, 1024)]
            RSTD = scr(2048, 1024)
            TMP = [scr(3072, 1024), scr(4096, 1024)]
            if not final:
                P.dve(lambda e: e.tensor_scalar_add(out=GS[:], in0=MOD[:, l, (3 * which + 1) * 16:(3 * which + 2) * 16, cnd],
                                                    scalar1=1.0), r=['MOD'], w=['GS'])
                P.dve(lambda e: e.tensor_mul(out=GS[:], in0=GS[:], in1=NORMG[:, l, which, :]), r=['GS', 'c_norm_gT'], w=['GS'])
            for kc in range(16):
                sq = SQ[kc % 2]
                P.act(lambda e, kc=kc, sq=sq: e.activation(out=sq.ap, in_=X[:, kc, :], func=AF.Square), r=[XK(kc)], w=[sq])
                for hv in range(2):
                    P.pe(lambda e, kc=kc, sq=sq, hv=hv: e.matmul(ps[hv][:], lhsT=ones_f[:], rhs=sq.ap[:, hv_sl(hv)],
                                                                 start=(kc == 0), stop=(kc == 15)), r=[sq, 'ones'], w=[PSK[hv]])
            for hv in range(2):
                P.act(lambda e, hv=hv: e.activation(out=RSTD.ap[:, hv_sl(hv)], in_=ps[hv][:], func=AF.Sqrt,
                                                    scale=1.0 / 2048, bias=EPS), r=[PSK[hv]], w=[RSTD])
            P.dve(lambda e: e.reciprocal(out=RSTD.ap, in_=RSTD.ap), r=[RSTD], w=[RSTD])
            for kc in range(16):
                tmp = TMP[kc % 2]
                P.dve(lambda e, kc=kc, tmp=tmp: e.tensor_mul(out=tmp.ap, in0=X[:, kc, :], in1=RSTD.ap), r=[XK(kc), RSTD], w=[tmp])
                if final:
                    P.act(lambda e, kc=kc, tmp=tmp: e.activation(out=tmp.ap, in_=tmp.ap, func=AF.Identity,
                                                                 scale=FNORM[:, kc:kc + 1]), r=[tmp, 'c_final_normT'], w=[tmp])
                    P.dma('sp', lambda e, kc=kc, tmp=tmp: e.dma_start(out=D["yg"][grp, :, kc, :], in_=tmp.ap), r=[tmp])
                else:
                    sh = MOD[:, l, (3 * which) * 16 + kc:(3 * which) * 16 + kc + 1, cnd]
                    P.act(lambda e, kc=kc, tmp=tmp, sh=sh: e.activation(out=H[:, kc, :], in_=tmp.ap, func=AF.Identity,
                                                                        scale=GS[:, kc:kc + 1], bias=sh),
                          r=[tmp, 'GS', 'MOD'], w=[HK(kc)])

        def phase_wout(wdram, l, cnd):
            wv = wdram.rearrange("(kc p) n -> p kc n", p=128)
            for blk in range(8):
                s = wload([(lambda s: wview(s, 16, 256), wv[:, :, blk * 256:(blk + 1) * 256])])
                for dci in range(2):
                    dc = blk * 2 + dci
                    gate = MOD[:, l, 32 + dc:32 + dc + 1, cnd]
                    for hv in range(2):
                        b = nbank()
                        for kc in range(16):
                            P.pe(lambda e, s=s, dci=dci, hv=hv, kc=kc, b=b: e.matmul(
                                ps[b][:], lhsT=wview(s, 16, 256)[:, kc, dci * 128:(dci + 1) * 128], rhs=MIX[:, kc, hv_sl(hv)],
                                start=(kc == 0), stop=(kc == 15)), r=wrk(s) + [('MIX', kc)], w=[PSK[b]])
                        P.dve(lambda e, dc=dc, hv=hv, b=b, gate=gate: e.scalar_tensor_tensor(
                            out=X[:, dc, hv_sl(hv)], in0=ps[b][:], scalar=gate, in1=X[:, dc, hv_sl(hv)],
                            op0=ALU.mult, op1=ALU.add), r=[PSK[b], XK(dc), 'MOD'], w=[XK(dc)])

        def phase_mlp(l, cnd):
            w1v = D["mlp_w1"][l].rearrange("(kc p) n -> p kc n", p=128)
            w2v = D["mlp_w2"][l].rearrange("(kc p) n -> p kc n", p=128)
            RL = [scr(0, 512), scr(512, 512)]
            rl_n = 0
            for hb in range(16):
                hid = mixf((hb % 2) * 2048, 2048, BF16)
                hidv = hid.ap.rearrange("p (k t) -> p k t", k=4)
                for sb_ in range(2):
                    c0 = hb * 512 + sb_ * 256
                    s = wload([(lambda s: wview(s, 16, 256), w1v[:, :, c0:c0 + 256])])
                    for mi in range(2):
                        for hv in range(2):
                            b = nbank()
                            for kc in range(16):
                                P.pe(lambda e, s=s, mi=mi, hv=hv, kc=kc, b=b: e.matmul(
                                    ps[b][:], lhsT=wview(s, 16, 256)[:, kc, mi * 128:(mi + 1) * 128], rhs=H[:, kc, hv_sl(hv)],
                                    start=(kc == 0), stop=(kc == 15)), r=wrk(s) + [HK(kc)], w=[PSK[b]])
                            rl = RL[rl_n % 2]
                            rl_n += 1
                            P.act(lambda e, b=b, rl=rl: e.activation(out=rl.ap, in_=ps[b][:], func=AF.Relu), r=[PSK[b]], w=[rl])
                            P.act(lambda e, rl=rl, k=sb_ * 2 + mi, hv=hv, hidv=hidv: e.activation(
                                out=hidv[:, k, hv_sl(hv)], in_=rl.ap, func=AF.Square), r=[rl], w=[hid])
                for half in range(2):
                    s = wload([(lambda s: wview(s, 4, 1024), w2v[:, hb * 4:(hb + 1) * 4, half * 1024:(half + 1) * 1024])])
                    for dci in range(8):
                        dc = half * 8 + dci
                        gate = MOD[:, l, 80 + dc:80 + dc + 1, cnd]
                        for hv in range(2):
                            b = nbank()
                            for k4 in range(4):
                                P.pe(lambda e, s=s, dci=dci, hv=hv, k4=k4, b=b, hidv=hidv: e.matmul(
                                    ps[b][:], lhsT=wview(s, 4, 1024)[:, k4, dci * 128:(dci + 1) * 128], rhs=hidv[:, k4, hv_sl(hv)],
                                    start=(k4 == 0), stop=(k4 == 3)), r=wrk(s) + [hid], w=[PSK[b]])
                            P.dve(lambda e, dc=dc, hv=hv, b=b, gate=gate: e.scalar_tensor_tensor(
                                out=X[:, dc, hv_sl(hv)], in0=ps[b][:], scalar=gate, in1=X[:, dc, hv_sl(hv)],
                                op0=ALU.mult, op1=ALU.add), r=[PSK[b], XK(dc), 'MOD'], w=[XK(dc)])

        def proj_fm(s, col_off, banks):
            for hv in range(2):
                b = banks[hv]
                for kc in range(16):
                    P.pe(lambda e, s=s, hv=hv, kc=kc, b=b, col_off=col_off: e.matmul(
                        ps[b][:], lhsT=wview(s, 16, 256)[:, kc, col_off:col_off + 128], rhs=H[:, kc, hv_sl(hv)],
                        start=(kc == 0), stop=(kc == 15)), r=wrk(s) + [HK(kc)], w=[PSK[b]])

        def attn_core(streams, KT, Vfn, scale, n_seq, L, ctx, ET, post):
            QB = min(L, 512)
            for sq_ in range(n_seq):
                kts = [(sq_ * L + j * 128, (sq_ * L) // 128 + j) for j in range(L // 128)]
                if ctx:
                    kts += [(1024 + j * 128, 8 + j) for j in range(4)]
                for qb in range(L // QB):
                    q0 = sq_ * L + qb * QB
                    nk = len(kts)
                    for idx, (kcol, kt) in enumerate(kts if 'ea_core' not in SKIP else []):
                        for i, (p0, p1, QT) in enumerate(streams):
                            sbk = i * 2 + (idx % 2)
                            E = ET[sbk]
                            P.pe(lambda e, sbk=sbk, p0=p0, p1=p1, kcol=kcol, QT=QT, q0=q0: e.matmul(
                                ps[sbk][:, :QB], lhsT=KT.ap[p0:p1, kcol:kcol + 128], rhs=QT.ap[p0:p1, q0:q0 + QB],
                                start=True, stop=True), r=[KT, QT], w=[PSK[sbk]])
                            P.act(lambda e, sbk=sbk, E=E: e.activation(out=E.ap[:, :QB], in_=ps[sbk][:, :QB], func=AF.Exp,
                                                                       scale=scale), r=[PSK[sbk]], w=[E])
                            vap, vk = Vfn(i, kt)
                            P.pe(lambda e, i=i, E=E, vap=vap, idx=idx: e.matmul(
                                ps[4 + 2 * i][:, :QB], lhsT=vap, rhs=E.ap[:, :QB], start=(idx == 0), stop=(idx == nk - 1)),
                                r=[E] + vk, w=[PSK[4 + 2 * i]])
                            P.pe(lambda e, i=i, E=E, idx=idx: e.matmul(
                                ps[5 + 2 * i][:, :QB], lhsT=ones_b[:], rhs=E.ap[:, :QB], start=(idx == 0), stop=(idx == nk - 1)),
                                r=[E, 'ones'], w=[PSK[5 + 2 * i]])
                    if 'ea_post' not in SKIP:
                        post(q0, QB)

        def rope_combine(b, hv, dst_ap, dstk, src_f32, QBb, T1, T2, COS, SIN, perm, sb7):
            srcap = src_f32.ap if src_f32 is not None else ps[b][:]
            srck = [src_f32] if src_f32 is not None else [PSK[b]]
            P.act(lambda e: e.copy(out=QBb.ap, in_=srcap), r=srck, w=[QBb])
            P.pe(lambda e: e.matmul(ps[sb7][:], lhsT=perm[:], rhs=QBb.ap, start=True, stop=True), r=[QBb, 'perm'], w=[PSK[sb7]])
            P.dve(lambda e: e.tensor_mul(out=T2.ap, in0=ps[sb7][:], in1=SIN.ap[:, hv_sl(hv)]), r=[PSK[sb7], SIN], w=[T2])
            P.dve(lambda e: e.tensor_mul(out=T1.ap, in0=srcap, in1=COS.ap[:, hv_sl(hv)]), r=srck + [COS], w=[T1])
            P.dve(lambda e: e.tensor_add(out=dst_ap, in0=T1.ap, in1=T2.ap), r=[T1, T2], w=dstk)

        def even_mixer(e_, l, grp, n_seq, L, ctx):
            win = D["even_w_in"][e_].rearrange("(kc p) n -> p kc n", p=128)
            lam_init = 0.8 - 0.6 * math.exp(-0.3 * l)
            for i in (range(2) if 'ea_lam' not in SKIP else []):
                P.dve(lambda e, i=i: e.tensor_mul(out=LT[:], in0=DLAM[:, e_, 2 * i, :], in1=DLAM[:, e_, 2 * i + 1, :]), r=['c_dlam'], w=['LT'])
                P.dve(lambda e, i=i: e.reduce_sum(out=LS[:, i:i + 1], in_=LT[:], axis=AX.X), r=['LT'], w=['LS'])
            P.act(lambda e: e.activation(out=LS[:, 0:2], in_=LS[:, 0:2], func=AF.Exp), r=['LS'], w=['LS'])
            P.dve(lambda e: e.tensor_sub(out=LS[:, 2:3], in0=LS[:, 1:2], in1=LS[:, 0:1]), r=['LS'], w=['LS'])
            P.dve(lambda e: e.tensor_scalar_add(out=LS[:, 2:3], in0=LS[:, 2:3], scalar1=-lam_init), r=['LS'], w=['LS'])
            P.dve(lambda e: e.tensor_scalar_mul(out=LS[:, 3:4], in0=SUBLN[:, e_:e_ + 1], scalar1=1.0 - lam_init), r=['c_sublnT'], w=['LS'])

            def conv():
                CP = [mixf(c * 1024, 1024) if c < 4 else scr((c - 4) * 1024, 1024) for c in range(8)]
                hw = n_seq * (L + 30)
                HHP = scr(4096, hw)
                HHPv = HHP.ap.rearrange("p (s l) -> p s l", s=n_seq)
                SIG = scr(5376, 1024)
                MEAN = scr(6400, 1024)
                R2 = scr(7424, 1024)
                P.dve(lambda e: e.memset(HHP.ap, 0.0), w=[HHP])
                for c in range(8):
                    s = wload([(lambda s: wview(s, 16, 256)[:, :, 0:128], win[:, :, 3072 + c * 128:3072 + (c + 1) * 128]),
                               (lambda s: wview(s, 16, 256)[:, :, 128:256], win[:, :, 4096 + c * 128:4096 + (c + 1) * 128])])
                    proj_fm(s, 0, [0, 1])
                    proj_fm(s, 128, [2, 3])
                    for hv in range(2):
                        P.act(lambda e, hv=hv: e.activation(out=SIG.ap[:, hv_sl(hv)], in_=ps[2 + hv][:], func=AF.Sigmoid),
                              r=[PSK[2 + hv]], w=[SIG])
                        if n_seq == 1:
                            dst = HHPv[:, 0, 15 + hv * 512:15 + hv * 512 + 512]
                            a_in = ps[hv][:]
                            s_in = SIG.ap[:, hv_sl(hv)]
                        else:
                            spp = 512 // L
                            dst = HHPv[:, hv * spp:(hv + 1) * spp, 15:15 + L]
                            a_in = ps[hv][:].rearrange("p (s l) -> p s l", s=spp)
                            s_in = SIG.ap[:, hv_sl(hv)].rearrange("p (s l) -> p s l", s=spp)
                        P.dve(lambda e, dst=dst, a_in=a_in, s_in=s_in: e.tensor_tensor(out=dst, in0=a_in, in1=s_in, op=ALU.mult),
                              r=[PSK[hv], SIG], w=[HHP])
                    acc = CP[c]
                    accv = acc.ap.rearrange("p (s l) -> p s l", s=n_seq)
                    P.dve(lambda e, c=c, accv=accv: e.tensor_scalar(out=accv, in0=HHPv[:, :, 0:L], scalar1=CONVW[:, e_, c, 0:1],
                                                                    scalar2=CONVB[:, e_, c:c + 1], op0=ALU.mult, op1=ALU.add),
                          r=[HHP, 'c_conv_wT', 'c_conv_bT'], w=[acc])
                    for k in range(1, 31):
                        P.dve(lambda e, c=c, k=k, accv=accv: e.scalar_tensor_tensor(
                            out=accv, in0=HHPv[:, :, k:k + L], scalar=CONVW[:, e_, c, k:k + 1], in1=accv, op0=ALU.mult, op1=ALU.add),
                            r=[HHP, acc, 'c_conv_wT'], w=[acc])
                for c in range(8):
                    P.act(lambda e, c=c: e.activation(out=SIG.ap, in_=CP[c].ap, func=AF.Square), r=[CP[c]], w=[SIG])
                    for hv in range(2):
                        P.pe(lambda e, c=c, hv=hv: e.matmul(ps[4 + hv][:], lhsT=ones_f[:], rhs=CP[c].ap[:, hv_sl(hv)],
                                                            start=(c == 0), stop=(c == 7)), r=[CP[c], 'ones'], w=[PSK[4 + hv]])
                        P.pe(lambda e, c=c, hv=hv: e.matmul(ps[6 + hv][:], lhsT=ones_f[:], rhs=SIG.ap[:, hv_sl(hv)],
                                                            start=(c == 0), stop=(c == 7)), r=[SIG, 'ones'], w=[PSK[6 + hv]])
                for hv in range(2):
                    P.act(lambda e, hv=hv: e.activation(out=MEAN.ap[:, hv_sl(hv)], in_=ps[4 + hv][:], func=AF.Identity, scale=1.0 / 1024),
                          r=[PSK[4 + hv]], w=[MEAN])
                    P.dve(lambda e, hv=hv: e.tensor_mul(out=R2.ap[:, hv_sl(hv)], in0=MEAN.ap[:, hv_sl(hv)], in1=MEAN.ap[:, hv_sl(hv)]),
                          r=[MEAN], w=[R2])
                    P.dve(lambda e, hv=hv: e.scalar_tensor_tensor(out=R2.ap[:, hv_sl(hv)], in0=ps[6 + hv][:], scalar=1.0 / 1024,
                                                                  in1=R2.ap[:, hv_sl(hv)], op0=ALU.mult, op1=ALU.subtract),
                          r=[PSK[6 + hv], R2], w=[R2])
                P.act(lambda e: e.activation(out=R2.ap, in_=R2.ap, func=AF.Sqrt, bias=EPS), r=[R2], w=[R2])
                P.dve(lambda e: e.reciprocal(out=R2.ap, in_=R2.ap), r=[R2], w=[R2])
                if os.environ.get("MK_DBG") == "conv" and grp == 0 and l == 0:
                    for i_, bb in enumerate([CP[0], CP[4], MEAN, R2, HHP]):
                        n_ = min(1024, bb.ap.shape[1])
                        P.dma('sp', lambda e, i_=i_, bb=bb, n_=n_: e.dma_start(out=D["dbg"][i_, :, 0:n_], in_=bb.ap[:, 0:n_]), r=[bb])
                for c in range(8):
                    P.dve(lambda e, c=c: e.tensor_sub(out=SIG.ap, in0=CP[c].ap, in1=MEAN.ap), r=[CP[c], MEAN], w=[SIG])
                    P.dve(lambda e: e.tensor_mul(out=SIG.ap, in0=SIG.ap, in1=R2.ap), r=[SIG, R2], w=[SIG])
                    P.act(lambda e, c=c: e.activation(out=MIX[:, 8 + c, :], in_=SIG.ap, func=AF.Silu, scale=CONVLN[:, e_, 0, c:c + 1],
                                                      bias=CONVLN[:, e_, 1, c:c + 1]), r=[SIG, 'c_conv_lnT'], w=[('MIX', 8 + c)])

            if 'conv' not in SKIP:
                conv()
            def eattn():
                KT = scr(0, 768, BF16)
                V2 = scr(768, 1536, BF16)
                V2v = V2.ap.rearrange("p (t c) -> p t c", t=12)
                QT = scr(2304, 512, BF16)
                QTb = scr(8192, 512, BF16)
                ET = [scr(2816 + i * 256, 256, BF16) for i in range(4)]
                T1 = scr(3840, 512); T2 = scr(4352, 512); QBb = scr(4864, 256, BF16)
                STG = [scr(5120, 512), scr(5632, 512)]
                COS = scr(6144, 1024); SIN = scr(7168, 1024)
                stg_n = [0]
                if ctx:
                    P.dma('sp', lambda e: e.dma_start(out=COS.ap, in_=D["ropeA"][0]), w=[COS])
                    P.dma('sp', lambda e: e.dma_start(out=SIN.ap, in_=D["ropeA"][1]), w=[SIN])
                for hp in range(4):
                    s = wload([(lambda s: wview(s, 16, 256), win[:, :, 2048 + hp * 256:2048 + (hp + 1) * 256])])
                    for tt in (range(8) if 'ea_v' not in SKIP else []):
                        b = nbank(0, 4)
                        for kc in range(16):
                            P.pe(lambda e, s=s, tt=tt, kc=kc, b=b: e.matmul(
                                ps[b][:, 0:256], lhsT=H[:, kc, tt * 128:(tt + 1) * 128], rhs=wview(s, 16, 256)[:, kc, :],
                                start=(kc == 0), stop=(kc == 15)), r=wrk(s) + [HK(kc)], w=[PSK[b]])
                        P.act(lambda e, tt=tt, b=b: e.copy(out=V2v[:, tt, :], in_=ps[b][:, 0:256]), r=[PSK[b]], w=[V2])
                        if not ctx and 'ea_oav' not in SKIP:
                            sg = STG[stg_n[0] % 2]; stg_n[0] += 1
                            P.act(lambda e, b=b, sg=sg: e.copy(out=sg.ap[:, 0:256], in_=ps[b][:, 0:256]), r=[PSK[b]], w=[sg])
                            P.dma('sp', lambda e, tt=tt, sg=sg, hp=hp: e.dma_start(
                                out=D["o_av"][e_, tt * 128:(tt + 1) * 128, hp * 256:(hp + 1) * 256], in_=sg.ap[:, 0:256]), r=[sg])
                    if ctx:
                        for hh in range(2):
                            P.dma('pool', lambda e, hh=hh, hp=hp: e.dma_start(
                                out=V2v[:, 8:12, hh * 128:(hh + 1) * 128],
                                in_=D["ca_v"][e_, 2 * hp + hh].rearrange("(j p) d -> p j d", p=128)), w=[V2])
                    for hh in range(2):
                        h = 2 * hp + hh
                        s = wload([(lambda s: wview(s, 16, 256)[:, :, 0:128], win[:, :, h * 128:(h + 1) * 128]),
                                   (lambda s: wview(s, 16, 256)[:, :, 128:256], win[:, :, 1024 + h * 128:1024 + (h + 1) * 128])])
                        proj_fm(s, 0, [0, 1])
                        proj_fm(s, 128, [2, 3])
                        for hv in (range(2) if 'ea_qk' not in SKIP else []):
                            if ctx:
                                rope_combine(hv, hv, QT.ap[:, hv_sl(hv)], [QT], None, QBb, T1, T2, COS, SIN, permA, 7)
                                rope_combine(2 + hv, hv, KT.ap[:, hv_sl(hv)], [KT], None, QBb, T1, T2, COS, SIN, permA, 7)
                            else:
                                P.act(lambda e, hv=hv: e.copy(out=QT.ap[:, hv_sl(hv)], in_=ps[hv][:]), r=[PSK[hv]], w=[QT])
                                P.act(lambda e, hv=hv: e.copy(out=KT.ap[:, hv_sl(hv)], in_=ps[2 + hv][:]), r=[PSK[2 + hv]], w=[KT])
                                if 'ea_oak' not in SKIP:
                                    sg = STG[stg_n[0] % 2]; stg_n[0] += 1
                                    P.act(lambda e, hv=hv, sg=sg: e.copy(out=sg.ap, in_=ps[2 + hv][:]), r=[PSK[2 + hv]], w=[sg])
                                    P.dma('sp', lambda e, hv=hv, sg=sg, h=h: e.dma_start(out=D["o_akT"][e_, h, :, hv_sl(hv)], in_=sg.ap), r=[sg])
                        if ctx:
                            P.dma('pool', lambda e, h=h: e.dma_start(out=KT.ap[:, 1024:1536], in_=D["ca_kT"][e_, h]), w=[KT])

                        def post(q0, QB, h=h):
                            P.dve(lambda e: e.reciprocal(out=T1.ap[:, :QB], in_=ps[5][:, :QB]), r=[PSK[5]], w=[T1])
                            P.dve(lambda e: e.tensor_mul(out=T1.ap[:, :QB], in0=ps[4][:, :QB], in1=T1.ap[:, :QB]), r=[PSK[4], T1], w=[T1])
                            P.dve(lambda e: e.reciprocal(out=T2.ap[:, :QB], in_=ps[7][:, :QB]), r=[PSK[7]], w=[T2])
                            P.dve(lambda e: e.tensor_mul(out=T2.ap[:, :QB], in0=ps[6][:, :QB], in1=T2.ap[:, :QB]), r=[PSK[6], T2], w=[T2])
                            P.dve(lambda e: e.scalar_tensor_tensor(out=T1.ap[:, :QB], in0=T2.ap[:, :QB], scalar=LS[:, 2:3], in1=T1.ap[:, :QB],
                                                                   op0=ALU.mult, op1=ALU.add), r=[T1, T2, 'LS'], w=[T1])
                            P.act(lambda e: e.activation(out=T2.ap[:, :QB], in_=T1.ap[:, :QB], func=AF.Square), r=[T1], w=[T2])
                            P.pe(lambda e: e.matmul(ps[0][:, :QB], lhsT=ones_f[:], rhs=T2.ap[:, :QB], start=True, stop=True),
                                 r=[T2, 'ones'], w=[PSK[0]])
                            P.act(lambda e: e.activation(out=T2.ap[:, :QB], in_=ps[0][:, :QB], func=AF.Sqrt, scale=1.0 / 128, bias=EPS),
                                  r=[PSK[0]], w=[T2])
                            P.dve(lambda e: e.reciprocal(out=T2.ap[:, :QB], in_=T2.ap[:, :QB]), r=[T2], w=[T2])
                            P.dve(lambda e: e.tensor_mul(out=T1.ap[:, :QB], in0=T1.ap[:, :QB], in1=T2.ap[:, :QB]), r=[T1, T2], w=[T1])
                            P.act(lambda e: e.activation(out=MIX[:, h, q0:q0 + QB], in_=T1.ap[:, :QB], func=AF.Identity, scale=LS[:, 3:4]),
                                  r=[T1, 'LS'], w=[('MIX', h)])

                        P.dve(lambda e: e.tensor_copy(out=QTb.ap, in_=QT.ap), r=[QT], w=[QTb])
                        P.dve(lambda e: e.memset(QTb.ap[0:64, :], 0.0), r=[QTb], w=[QTb])
                        P.dve(lambda e: e.memset(QT.ap[64:128, :], 0.0), r=[QT, QTb], w=[QT])
                        attn_core([(0, 128, QT), (0, 128, QTb)], KT,
                                  lambda i, kt, hh=hh: (V2v[:, kt, hh * 128:(hh + 1) * 128], [V2]),
                                  A_SCALE, n_seq, L, ctx, ET, post)

            if 'eattn' not in SKIP:
                eattn()

        A_SCALE = 64 ** -0.5
        D_SCALE = 128 ** -0.5

        def odd_mixer(o, l, grp, n_seq, L, ctx):
            win = D["odd_w_in"][o].rearrange("(kc p) n -> p kc n", p=128)
            SPK = 'SP'
            def sp(i):
                return SP[:, i, :]
            are, aim, ldt = ARE[:, o, :], AIM[:, o, :], LDT[:, o, :]
            rd = ['c_a_reT', 'c_a_imT', 'c_ldtT', 'c_h0T', SPK]
            P.act(lambda e: e.activation(out=sp(7), in_=ldt, func=AF.Exp), r=rd, w=[SPK])
            P.dve(lambda e: e.tensor_mul(out=sp(0), in0=are, in1=sp(7)), r=rd, w=[SPK])
            P.act(lambda e: e.activation(out=sp(0), in_=sp(0), func=AF.Exp), r=rd, w=[SPK])
            P.dve(lambda e: e.tensor_mul(out=sp(1), in0=aim, in1=sp(7)), r=rd, w=[SPK])
            P.dve(lambda e: e.tensor_scalar_mul(out=sp(1), in0=sp(1), scalar1=float(1.0 / (2 * math.pi))), r=rd, w=[SPK])
            for (dsti, off) in ((3, 0.0), (2, 0.25)):
                P.dve(lambda e, off=off: e.tensor_scalar_add(out=sp(8), in0=sp(1), scalar1=off), r=rd, w=[SPK])
                P.dve(lambda e: e.tensor_copy(out=SPI[:], in_=sp(8)), r=rd, w=[SPK])
                P.dve(lambda e: e.tensor_copy(out=sp(11), in_=SPI[:]), r=rd, w=[SPK])
                P.dve(lambda e: e.tensor_sub(out=sp(8), in0=sp(8), in1=sp(11)), r=rd, w=[SPK])
                P.act(lambda e, dsti=dsti: e.activation(out=sp(dsti), in_=sp(8), func=AF.Sin, scale=6.283185), r=rd, w=[SPK])
            P.dve(lambda e: e.tensor_mul(out=sp(2), in0=sp(2), in1=sp(0)), r=rd, w=[SPK])
            P.dve(lambda e: e.tensor_mul(out=sp(3), in0=sp(3), in1=sp(0)), r=rd, w=[SPK])
            P.dve(lambda e: e.tensor_mul(out=sp(7), in0=are, in1=are), r=rd, w=[SPK])
            P.dve(lambda e: e.tensor_mul(out=sp(8), in0=aim, in1=aim), r=rd, w=[SPK])
            P.dve(lambda e: e.tensor_add(out=sp(7), in0=sp(7), in1=sp(8)), r=rd, w=[SPK])
            P.dve(lambda e: e.reciprocal(out=sp(7), in_=sp(7)), r=rd, w=[SPK])
            P.dve(lambda e: e.tensor_scalar_add(out=sp(8), in0=sp(2), scalar1=-1.0), r=rd, w=[SPK])
            P.dve(lambda e: e.tensor_mul(out=sp(4), in0=sp(8), in1=are), r=rd, w=[SPK])
            P.dve(lambda e: e.tensor_mul(out=sp(11), in0=sp(3), in1=aim), r=rd, w=[SPK])
            P.dve(lambda e: e.tensor_add(out=sp(4), in0=sp(4), in1=sp(11)), r=rd, w=[SPK])
            P.dve(lambda e: e.tensor_mul(out=sp(4), in0=sp(4), in1=sp(7)), r=rd, w=[SPK])
            P.dve(lambda e: e.tensor_mul(out=sp(5), in0=sp(3), in1=are), r=rd, w=[SPK])
            P.dve(lambda e: e.tensor_mul(out=sp(11), in0=sp(8), in1=aim), r=rd, w=[SPK])
            P.dve(lambda e: e.tensor_sub(out=sp(5), in0=sp(5), in1=sp(11)), r=rd, w=[SPK])
            P.dve(lambda e: e.tensor_mul(out=sp(5), in0=sp(5), in1=sp(7)), r=rd, w=[SPK])
            P.dve(lambda e: e.tensor_scalar_mul(out=sp(6), in0=sp(5), scalar1=-1.0), r=rd, w=[SPK])
            if ctx:
                h0re, h0im = H0[:, o, 0, :], H0[:, o, 1, :]
                P.dve(lambda e: e.tensor_mul(out=sp(9), in0=sp(2), in1=h0re), r=rd, w=[SPK])
                P.dve(lambda e: e.tensor_mul(out=sp(11), in0=sp(3), in1=h0im), r=rd, w=[SPK])
                P.dve(lambda e: e.tensor_sub(out=sp(9), in0=sp(9), in1=sp(11)), r=rd, w=[SPK])
                P.dve(lambda e: e.tensor_mul(out=sp(10), in0=sp(2), in1=h0im), r=rd, w=[SPK])
                P.dve(lambda e: e.tensor_mul(out=sp(11), in0=sp(3), in1=h0re), r=rd, w=[SPK])
                P.dve(lambda e: e.tensor_add(out=sp(10), in0=sp(10), in1=sp(11)), r=rd, w=[SPK])

            def s5():
                U = scr(0, 512, BF16)
                BRE = scr(512, 1024); BIM = scr(1536, 1024); GRE = scr(2560, 1024); GIM = scr(3584, 1024)
                HRE = scr(512, 512, BF16); HIM = scr(1536, 512, BF16); T1P = scr(1024, 512); T2P = scr(2048, 512)
                BL = scr(4608, 1024, BF16); CL = scr(5632, 1024, BF16)
                BLv = BL.ap.rearrange("p (a c) -> p a c", a=16); CLv = CL.ap.rearrange("p (a c) -> p a c", a=16)
                BZ = scr(6656, 256); CZ = scr(6912, 256)
                BZv = BZ.ap.rearrange("p (a c) -> p a c", a=2); CZv = CZ.ap.rearrange("p (a c) -> p a c", a=2)
                BT1 = scr(7168, 128); BT2 = scr(7424, 128)
                T1 = scr(7680, 512); T2 = scr(8192, 512)
                KF = Buf(SCR[:, 7680:8704], T1.keys + T2.keys)
                FS = scr(8704, 512)
                FSv = FS.ap.rearrange("p (d r j s) -> p d r j s", d=2, r=2, j=32)
                COS = mixf(0, 1024); SIN = mixf(1024, 1024); DU = mixf(2048, 1024)
                YT = GRE
                IT = sub(GIM, GIM.ap.bitcast(I32))
                YG = lambda c: ('MIX', 8 + c)
                TVt = mixf(3072, 1024)
                P.dma('sp', lambda e: e.dma_start(out=TVt.ap, in_=D["tvec"]), w=[TVt])
                segs = []
                for hv in range(2):
                    if n_seq == 1:
                        segs.append((hv * 512, 512, 0, hv))
                    else:
                        for q in range(512 // L):
                            sq_ = hv * (512 // L) + q
                            segs.append((sq_ * L, L, sq_, hv))

                def views(buf_ap, n0, ln, sq_, d):
                    if d == 0:
                        return buf_ap[:, n0:n0 + ln]
                    e0 = (2 * sq_ + 1) * L - 1 - n0
                    return buf_ap[:, e0 - ln + 1:e0 + 1][:, ::-1]

                def tviews(tab_ap, n0, ln, sq_, d):
                    t0 = n0 - sq_ * L
                    if d == 0:
                        return tab_ap[:, t0:t0 + ln]
                    tp0 = L - 1 - t0
                    return tab_ap[:, tp0 - ln + 1:tp0 + 1][:, ::-1]

                for c in range(8):
                    s = wload([(lambda s: wview(s, 16, 256)[:, :, 0:128], win[:, :, c * 128:(c + 1) * 128])])
                    proj_fm(s, 0, [0, 1])
                    for hv in range(2):
                        P.act(lambda e, hv=hv: e.copy(out=U.ap[:, hv_sl(hv)], in_=ps[hv][:]), r=[PSK[hv]], w=[U])
                        P.act(lambda e, hv=hv, c=c: e.activation(out=DU.ap[:, hv_sl(hv)], in_=ps[hv][:], func=AF.Identity,
                                                                 scale=SSMD[:, o, c:c + 1]), r=[PSK[hv], 'c_ssm_dT'], w=[DU])
                    first = True
                    for d in range(2):
                        for j in range(4):
                            jt = 4 * c + j
                            idx = d * 32 + jt
                            a = (d * 4 + j) * 2
                            P.dma('sp', lambda e, d=d, jt=jt: e.dma_start(out=BZv, in_=D["ssm_bz"][o, d, jt]), w=[BZ])
                            P.dma('sp', lambda e, d=d, jt=jt: e.dma_start(out=CZv, in_=D["ssm_cz"][o, d, jt]), w=[CZ])
                            kre, kim, kimn = SP[:, 4, idx:idx + 1], SP[:, 5, idx:idx + 1], SP[:, 6, idx:idx + 1]
                            P.dve(lambda e, kre=kre: e.tensor_scalar_mul(out=BT1.ap, in0=BZv[:, 0, :], scalar1=kre), r=[BZ, SPK], w=[BT1])
                            P.dve(lambda e, kimn=kimn: e.scalar_tensor_tensor(out=BT1.ap, in0=BZv[:, 1, :], scalar=kimn, in1=BT1.ap,
                                                                              op0=ALU.mult, op1=ALU.add), r=[BZ, BT1, SPK], w=[BT1])
                            P.dve(lambda e, kre=kre: e.tensor_scalar_mul(out=BT2.ap, in0=BZv[:, 1, :], scalar1=kre), r=[BZ, SPK], w=[BT2])
                            P.dve(lambda e, kim=kim: e.scalar_tensor_tensor(out=BT2.ap, in0=BZv[:, 0, :], scalar=kim, in1=BT2.ap,
                                                                            op0=ALU.mult, op1=ALU.add), r=[BZ, BT2, SPK], w=[BT2])
                            P.pe(lambda e: e.transpose(out=ps[6][:, 0:128], in_=BT1.ap, identity=ident[:]), r=[BT1, 'c_ident'], w=[PSK[6]])
                            P.pe(lambda e: e.transpose(out=ps[7][:, 0:128], in_=BT2.ap, identity=ident[:]), r=[BT2, 'c_ident'], w=[PSK[7]])
                            P.act(lambda e, a=a: e.copy(out=BLv[:, a, :], in_=ps[6][:, 0:128]), r=[PSK[6]], w=[BL])
                            P.act(lambda e, a=a: e.copy(out=BLv[:, a + 1, :], in_=ps[7][:, 0:128]), r=[PSK[7]], w=[BL])
                            P.act(lambda e, a=a: e.copy(out=CLv[:, a, :], in_=CZv[:, 0, :]), r=[CZ], w=[CL])
                            P.act(lambda e, a=a: e.activation(out=CLv[:, a + 1, :], in_=CZv[:, 1, :], func=AF.Identity, scale=-1.0),
                                  r=[CZ], w=[CL])
                            for hv in range(2):
                                P.pe(lambda e, a=a, hv=hv: e.matmul(ps[hv][:], lhsT=BLv[:, a, :], rhs=U.ap[:, hv_sl(hv)], start=True, stop=True),
                                     r=[BL, U], w=[PSK[hv]])
                                P.pe(lambda e, a=a, hv=hv: e.matmul(ps[2 + hv][:], lhsT=BLv[:, a + 1, :], rhs=U.ap[:, hv_sl(hv)], start=True, stop=True),
                                     r=[BL, U], w=[PSK[2 + hv]])
                            thn = SP[:, 1, idx:idx + 1]
                            P.dve(lambda e, thn=thn: e.tensor_scalar_mul(out=YT.ap[:, 0:L], in0=TVt.ap[:, 0:L], scalar1=thn), r=[TVt, SPK], w=[YT])
                            for (tab, off) in ((SIN, 0.0), (COS, 0.25)):
                                if off:
                                    P.dve(lambda e, off=off: e.tensor_scalar_add(out=YT.ap[:, 0:L], in0=YT.ap[:, 0:L], scalar1=off), r=[YT], w=[YT])
                                P.dve(lambda e: e.tensor_copy(out=IT.ap[:, 0:L], in_=YT.ap[:, 0:L]), r=[YT], w=[IT])
                                P.dve(lambda e: e.tensor_copy(out=KF.ap[:, 0:L], in_=IT.ap[:, 0:L]), r=[IT], w=[KF])
                                P.dve(lambda e: e.tensor_sub(out=KF.ap[:, 0:L], in0=YT.ap[:, 0:L], in1=KF.ap[:, 0:L]), r=[YT, KF], w=[KF])
                                P.act(lambda e, tab=tab: e.activation(out=tab.ap[:, 0:L], in_=KF.ap[:, 0:L], func=AF.Sin, scale=6.283185),
                                      r=[KF], w=[tab])
                            for (n0, ln, sq_, hv) in segs:
                                c0 = n0 - hv * 512
                                A_ = ps[hv][:, c0:c0 + ln]; B_ = ps[2 + hv][:, c0:c0 + ln]
                                Cc = tviews(COS.ap, n0, ln, sq_, d); Sn = tviews(SIN.ap, n0, ln, sq_, d)
                                ore = views(BRE.ap, n0, ln, sq_, d); oim = views(BIM.ap, n0, ln, sq_, d)
                                t1 = T1.ap[:, 0:ln]
                                P.dve(lambda e, ore=ore, A_=A_, Cc=Cc: e.tensor_mul(out=ore, in0=A_, in1=Cc), r=[PSK[hv], COS], w=[BRE])
                                P.dve(lambda e, t1=t1, B_=B_, Sn=Sn: e.tensor_mul(out=t1, in0=B_, in1=Sn), r=[PSK[2 + hv], SIN], w=[T1])
                                P.dve(lambda e, ore=ore, t1=t1: e.tensor_add(out=ore, in0=ore, in1=t1), r=[BRE, T1], w=[BRE])
                                P.dve(lambda e, oim=oim, B_=B_, Cc=Cc: e.tensor_mul(out=oim, in0=B_, in1=Cc), r=[PSK[2 + hv], COS], w=[BIM])
                                P.dve(lambda e, t1=t1, A_=A_, Sn=Sn: e.tensor_mul(out=t1, in0=A_, in1=Sn), r=[PSK[hv], SIN], w=[T1])
                                P.dve(lambda e, oim=oim, t1=t1: e.tensor_sub(out=oim, in0=oim, in1=t1), r=[BIM, T1], w=[BIM])
                            if ctx:
                                P.dve(lambda e, idx=idx: e.tensor_add(out=BRE.ap[:, 0:1], in0=BRE.ap[:, 0:1], in1=SP[:, 9, idx:idx + 1]), r=[BRE, SPK], w=[BRE])
                                P.dve(lambda e, idx=idx: e.tensor_add(out=BIM.ap[:, 0:1], in0=BIM.ap[:, 0:1], in1=SP[:, 10, idx:idx + 1]), r=[BIM, SPK], w=[BIM])
                            mag = SP[:, 0, idx:idx + 1]
                            for sq_ in range(n_seq):
                                sl = slice(sq_ * L, (sq_ + 1) * L)
                                P.dve(lambda e, sl=sl, mag=mag: e.tensor_tensor_scan(out=GRE.ap[:, sl], data0=mag.broadcast_to([128, L]),
                                                                                     data1=BRE.ap[:, sl], initial=0.0, op0=ALU.mult, op1=ALU.add),
                                      r=[BRE, SPK], w=[GRE])
                                P.dve(lambda e, sl=sl, mag=mag: e.tensor_tensor_scan(out=GIM.ap[:, sl], data0=mag.broadcast_to([128, L]),
                                                                                     data1=BIM.ap[:, sl], initial=0.0, op0=ALU.mult, op1=ALU.add),
                                      r=[BIM, SPK], w=[GIM])
                            if not ctx:
                                gl_re = GRE.ap[:, L - 1::L]; gl_im = GIM.ap[:, L - 1::L]
                                cl, sl_ = COS.ap[:, L - 1:L], SIN.ap[:, L - 1:L]
                                fre = FSv[:, d, 0, jt, :]; fim = FSv[:, d, 1, jt, :]
                                tq = T1.ap[:, 0:4]
                                P.dve(lambda e, gl_im=gl_im, sl_=sl_, tq=tq: e.tensor_scalar_mul(out=tq, in0=gl_im, scalar1=sl_), r=[GIM, SIN], w=[T1])
                                P.dve(lambda e, gl_re=gl_re, cl=cl, tq=tq, fre=fre: e.scalar_tensor_tensor(
                                    out=fre, in0=gl_re, scalar=cl, in1=tq, op0=ALU.mult, op1=ALU.subtract), r=[GRE, COS, T1], w=[FS])
                                P.dve(lambda e, gl_re=gl_re, sl_=sl_, tq=tq: e.tensor_scalar_mul(out=tq, in0=gl_re, scalar1=sl_), r=[GRE, SIN], w=[T1])
                                P.dve(lambda e, gl_im=gl_im, cl=cl, tq=tq, fim=fim: e.scalar_tensor_tensor(
                                    out=fim, in0=gl_im, scalar=cl, in1=tq, op0=ALU.mult, op1=ALU.add), r=[GIM, COS, T1], w=[FS])
                            for (n0, ln, sq_, hv) in segs:
                                Cc = tviews(COS.ap, n0, ln, sq_, d); Sn = tviews(SIN.ap, n0, ln, sq_, d)
                                gr = views(GRE.ap, n0, ln, sq_, d); gi = views(GIM.ap, n0, ln, sq_, d)
                                t1 = T1.ap[:, 0:ln]; t2 = T2.ap[:, 0:ln]
                                t1p = T1P.ap[:, 0:ln]; t2p = T2P.ap[:, 0:ln]
                                P.add('pool', lambda e, t1p=t1p, gr=gr, Cc=Cc: e.tensor_tensor(out=t1p, in0=gr, in1=Cc, op=ALU.mult), r=[GRE, COS], w=[T1P])
                                P.add('pool', lambda e, t2p=t2p, gi=gi, Sn=Sn: e.tensor_tensor(out=t2p, in0=gi, in1=Sn, op=ALU.mult), r=[GIM, SIN], w=[T2P])
                                P.add('pool', lambda e, t1p=t1p, t2p=t2p, n0=n0, ln=ln: e.tensor_tensor(out=HRE.ap[:, n0:n0 + ln], in0=t1p, in1=t2p, op=ALU.subtract),
                                      r=[T1P, T2P], w=[HRE])
                                P.dve(lambda e, t1=t1, gr=gr, Sn=Sn: e.tensor_mul(out=t1, in0=gr, in1=Sn), r=[GRE, SIN], w=[T1])
                                P.dve(lambda e, t2=t2, gi=gi, Cc=Cc: e.tensor_mul(out=t2, in0=gi, in1=Cc), r=[GIM, COS], w=[T2])
                                P.dve(lambda e, t1=t1, t2=t2, n0=n0, ln=ln: e.tensor_add(out=HIM.ap[:, n0:n0 + ln], in0=t1, in1=t2), r=[T1, T2], w=[HIM])
                            last = (d == 1 and j == 3)
                            for hv in range(2):
                                P.pe(lambda e, a=a, hv=hv, first=first: e.matmul(ps[4 + hv][:], lhsT=CLv[:, a, :], rhs=HRE.ap[:, hv_sl(hv)],
                                                                                start=first, stop=False), r=[CL, HRE], w=[PSK[4 + hv]])
                                P.pe(lambda e, a=a, hv=hv, last=last: e.matmul(ps[4 + hv][:], lhsT=CLv[:, a + 1, :], rhs=HIM.ap[:, hv_sl(hv)],
                                                                              start=False, stop=last), r=[CL, HIM], w=[PSK[4 + hv]])
                            first = False
                    for hv in range(2):
                        P.dve(lambda e, hv=hv: e.tensor_add(out=T1.ap, in0=ps[4 + hv][:], in1=DU.ap[:, hv_sl(hv)]), r=[PSK[4 + hv], DU], w=[T1])
                        P.act(lambda e: e.activation(out=T2.ap, in_=T1.ap, func=AF.Square), r=[T1], w=[T2])
                        P.dve(lambda e: e.tensor_scalar(out=T2.ap, in0=T2.ap, scalar1=0.044715, scalar2=1.0, op0=ALU.mult, op1=ALU.add), r=[T2], w=[T2])
                        P.dve(lambda e: e.tensor_mul(out=T2.ap, in0=T2.ap, in1=T1.ap), r=[T1, T2], w=[T2])
                        P.act(lambda e: e.activation(out=T2.ap, in_=T2.ap, func=AF.Sigmoid, scale=1.5957691216057308), r=[T2], w=[T2])
                        P.dve(lambda e, hv=hv, c=c: e.tensor_mul(out=MIX[:, 8 + c, hv_sl(hv)], in0=T2.ap, in1=T1.ap), r=[T1, T2], w=[YG(c)])
                if not ctx:
                    P.dma('sp', lambda e: e.dma_start(out=D["o_ssm"][o], in_=FSv), r=[FS])
                gv = D["glu_w"][o].rearrange("(kc p) n -> p kc n", p=128)
                for blk in range(2):
                    s = wload([(lambda s: wview(s, 8, 512), gv[:, :, blk * 512:(blk + 1) * 512])])
                    for mi in range(4):
                        m = blk * 4 + mi
                        for hv in range(2):
                            b = nbank(0, 4)
                            for kc in range(8):
                                P.pe(lambda e, s=s, mi=mi, hv=hv, kc=kc, b=b: e.matmul(
                                    ps[b][:], lhsT=wview(s, 8, 512)[:, kc, mi * 128:(mi + 1) * 128], rhs=MIX[:, 8 + kc, hv_sl(hv)],
                                    start=(kc == 0), stop=(kc == 7)), r=wrk(s) + [YG(kc)], w=[PSK[b]])
                            P.act(lambda e, b=b, m=m: e.activation(out=T2.ap, in_=ps[b][:], func=AF.Sigmoid, bias=GLUB[:, o, m:m + 1]),
                                  r=[PSK[b], 'c_glu_bT'], w=[T2])
                            P.dve(lambda e, m=m, hv=hv: e.tensor_mul(out=MIX[:, m, hv_sl(hv)], in0=MIX[:, 8 + m, hv_sl(hv)], in1=T2.ap),
                                  r=[YG(m), T2], w=[('MIX', m)])

            if 's5' not in SKIP:
                s5()
            def gqa():
                KT = scr(0, 768, BF16)
                V1 = scr(768, 768, BF16)
                V1v = V1.ap.rearrange("p (t c) -> p t c", t=12)
                QT2 = [scr(1536, 512, BF16), scr(2048, 512, BF16)]
                ET = [scr(2560 + i * 256, 256, BF16) for i in range(4)]
                T1 = scr(3584, 512); T2 = scr(4096, 512); QBb = scr(4608, 256, BF16)
                STG = [scr(4864, 512), scr(5376, 512)]
                COS = scr(5888, 1024); SIN = scr(6912, 1024)
                QF = scr(7936, 512)
                stg_n = [0]
                if ctx:
                    P.dma('sp', lambda e: e.dma_start(out=COS.ap, in_=D["ropeD"][0]), w=[COS])
                    P.dma('sp', lambda e: e.dma_start(out=SIN.ap, in_=D["ropeD"][1]), w=[SIN])

                def norm_rope(b, hv, gcol, dst_ap, dstk, out_dram=None):
                    P.act(lambda e: e.activation(out=T2.ap, in_=ps[b][:], func=AF.Square), r=[PSK[b]], w=[T2])
                    P.pe(lambda e: e.matmul(ps[6][:], lhsT=ones_f[:], rhs=T2.ap, start=True, stop=True), r=[T2, 'ones'], w=[PSK[6]])
                    P.act(lambda e: e.activation(out=T2.ap, in_=ps[6][:], func=AF.Sqrt, scale=1.0 / 128, bias=EPS), r=[PSK[6]], w=[T2])
                    P.dve(lambda e: e.reciprocal(out=T2.ap, in_=T2.ap), r=[T2], w=[T2])
                    P.dve(lambda e: e.tensor_mul(out=QF.ap, in0=ps[b][:], in1=T2.ap), r=[PSK[b], T2], w=[QF])
                    if ctx:
                        P.act(lambda e: e.activation(out=QF.ap, in_=QF.ap, func=AF.Identity, scale=gcol), r=[QF, 'c_qk_normT'], w=[QF])
                        rope_combine(b, hv, dst_ap, dstk, QF, QBb, T1, T2, COS, SIN, permD, 7)
                    else:
                        P.act(lambda e: e.activation(out=dst_ap, in_=QF.ap, func=AF.Identity, scale=gcol), r=[QF, 'c_qk_normT'], w=dstk)
                        if out_dram is not None:
                            sg = STG[stg_n[0] % 2]; stg_n[0] += 1
                            P.act(lambda e, sg=sg: e.activation(out=sg.ap, in_=QF.ap, func=AF.Identity, scale=gcol), r=[QF, 'c_qk_normT'], w=[sg])
                            P.dma('sp', lambda e, sg=sg: e.dma_start(out=out_dram, in_=sg.ap), r=[sg])

                for g in range(2):
                    s = wload([(lambda s: wview(s, 16, 256)[:, :, 0:128], win[:, :, 2048 + g * 128:2048 + (g + 1) * 128]),
                               (lambda s: wview(s, 16, 256)[:, :, 128:256], win[:, :, 2304 + g * 128:2304 + (g + 1) * 128])])
                    proj_fm(s, 0, [0, 1])
                    for hv in range(2):
                        norm_rope(hv, hv, QKN[:, o, 1:2], KT.ap[:, hv_sl(hv)], [KT],
                                  None if ctx else D["o_dkT"][o, g, :, hv_sl(hv)])
                    for tt in range(8):
                        b = nbank(2, 4)
                        for kc in range(16):
                            P.pe(lambda e, s=s, tt=tt, kc=kc, b=b: e.matmul(
                                ps[b][:, 0:128], lhsT=H[:, kc, tt * 128:(tt + 1) * 128], rhs=wview(s, 16, 256)[:, kc, 128:256],
                                start=(kc == 0), stop=(kc == 15)), r=wrk(s) + [HK(kc)], w=[PSK[b]])
                        P.act(lambda e, tt=tt, b=b: e.copy(out=V1v[:, tt, :], in_=ps[b][:, 0:128]), r=[PSK[b]], w=[V1])
                        if not ctx:
                            sg = STG[stg_n[0] % 2]; stg_n[0] += 1
                            P.act(lambda e, b=b, sg=sg: e.copy(out=sg.ap[:, 0:128], in_=ps[b][:, 0:128]), r=[PSK[b]], w=[sg])
                            P.dma('sp', lambda e, tt=tt, sg=sg, g=g: e.dma_start(
                                out=D["o_dv"][o, tt * 128:(tt + 1) * 128, g * 128:(g + 1) * 128], in_=sg.ap[:, 0:128]), r=[sg])
                    if ctx:
                        P.dma('pool', lambda e, g=g: e.dma_start(out=KT.ap[:, 1024:1536], in_=D["cd_kT"][o, g]), w=[KT])
                        P.dma('pool', lambda e, g=g: e.dma_start(out=V1v[:, 8:12, :],
                                                                 in_=D["cd_v"][o, g].rearrange("(j p) d -> p j d", p=128)), w=[V1])
                    for rp in range(2):
                        h0_ = g * 4 + rp * 2
                        s = wload([(lambda s: wview(s, 16, 256), win[:, :, 1024 + h0_ * 128:1024 + (h0_ + 2) * 128])])
                        for i in range(2):
                            proj_fm(s, i * 128, [0, 1])
                            for hv in range(2):
                                norm_rope(hv, hv, QKN[:, o, 0:1], QT2[i].ap[:, hv_sl(hv)], [QT2[i]])

                        def post(q0, QB, h0_=h0_):
                            for i in range(2):
                                tq = T1 if i == 0 else T2
                                P.dve(lambda e, i=i, tq=tq: e.reciprocal(out=tq.ap[:, :QB], in_=ps[5 + 2 * i][:, :QB]), r=[PSK[5 + 2 * i]], w=[tq])
                                P.dve(lambda e, i=i, tq=tq: e.tensor_mul(out=MIX[:, 8 + h0_ + i, q0:q0 + QB], in0=ps[4 + 2 * i][:, :QB],
                                                                         in1=tq.ap[:, :QB]), r=[PSK[4 + 2 * i], tq], w=[('MIX', 8 + h0_ + i)])

                        attn_core([(0, 128, QT2[0]), (0, 128, QT2[1])], KT, lambda i, kt: (V1v[:, kt, :], [V1]),
                                  D_SCALE, n_seq, L, ctx, ET, post)

            if 'gqa' not in SKIP:
                gqa()

        for grp in range(2):
            if ('grp%d' % grp) in SKIP:
                continue
            cnd = grp
            n_seq, L, ctx = (1, 1024, True) if grp == 0 else (4, 256, False)
            for kc in range(16):
                P.dma('sp', lambda e, kc=kc, grp=grp: e.dma_start(out=X[:, kc, :], in_=D["xg"][grp, :, kc, :]), w=[XK(kc)])
            for l in range(NLAYERS):
                phase_norm(l, 0, cnd)
                if l % 2 == 0:
                    if 'even' not in SKIP:
                        even_mixer(l // 2, l, grp, n_seq, L, ctx)
                    if os.environ.get("MK_DBG") == "mix" and grp == 0 and l == 0:
                        for i_, c_ in enumerate([0, 3, 7, 8, 12, 15]):
                            tb = scr(0, 1024)
                            P.dve(lambda e, c_=c_, tb=tb: e.tensor_copy(out=tb.ap, in_=MIX[:, c_, :]), r=[('MIX', c_)], w=[tb])
                            P.dma('sp', lambda e, i_=i_, tb=tb: e.dma_start(out=D["dbg"][i_], in_=tb.ap), r=[tb])
                    if 'wout' not in SKIP:
                        phase_wout(D["even_w_out"][l // 2], l, cnd)
                else:
                    if 'odd' not in SKIP:
                        odd_mixer(l // 2, l, grp, n_seq, L, ctx)
                    if 'wout' not in SKIP:
                        phase_wout(D["odd_w_out"][l // 2], l, cnd)
                phase_norm(l, 1, cnd)
                if 'mlp' not in SKIP:
                    phase_mlp(l, cnd)
            phase_norm(0, 0, cnd, final=True, grp=grp)
        P.emit()
    return nc


NLAYERS = int(os.environ.get("MK_NLAYERS", "4"))
SKIP = set(os.environ.get("MK_SKIP", "").split(","))
NCORES = int(os.environ.get("MK_NCORES", "8"))

_CACHE = {}


def _rope_tables(dim, nheads_rep):
    GRID_W = 64
    rows = TT // GRID_W
    row = np.repeat(np.arange(rows, dtype=np.float32), GRID_W)
    col = np.tile(np.arange(GRID_W, dtype=np.float32), rows)
    quarter = dim // 4
    inv_freq = (np.float32(10000.0) ** (-np.arange(quarter, dtype=np.float32) / np.float32(quarter))).astype(np.float32)
    ang_r = row[:, None] * inv_freq[None, :]
    ang_c = col[:, None] * inv_freq[None, :]
    ang = np.concatenate([ang_r, ang_r, ang_c, ang_c], axis=-1).astype(np.float32)
    cos = np.cos(ang).astype(np.float32).T
    sin = np.sin(ang).astype(np.float32).T
    sign = np.ones((dim, 1), np.float32)
    sign[0:quarter] = -1
    sign[2 * quarter:3 * quarter] = -1
    sinS = sin * sign
    perm = np.zeros((dim, dim), np.float32)
    for m in range(dim):
        q = m // quarter
        sig = m + quarter if q % 2 == 0 else m - quarter
        perm[sig, m] = 1
    cos = np.tile(cos, (nheads_rep, 1))
    sinS = np.tile(sinS, (nheads_rep, 1))
    permf = np.zeros((dim * nheads_rep, dim * nheads_rep), np.float32)
    for r in range(nheads_rep):
        permf[r * dim:(r + 1) * dim, r * dim:(r + 1) * dim] = perm
    return np.ascontiguousarray(np.stack([cos, sinS])), permf


def _fm(v):
    lead = v.shape[:-1]
    n = v.shape[-1] // 128
    a = v.reshape(*lead, n, 128)
    return np.ascontiguousarray(np.moveaxis(a, -1, 0))


def kernel(**inp):
    inp = {k: np.asarray(v, dtype=np.float32) for k, v in inp.items()}
    if "nc" not in _CACHE:
        _CACHE["nc"] = build()
    nc = _CACHE["nc"]
    f32 = np.float32
    ropeA, permA = _rope_tables(64, 2)
    ropeD, permD = _rope_tables(128, 1)
    shared = {
        "ada_w": inp["ada_w"], "ada_bT": _fm(inp["ada_b"]),
        "norm_gT": _fm(inp["norm_g"]), "final_normT": _fm(inp["final_norm"]),
        "mlp_w1": inp["mlp_w1"], "mlp_w2": inp["mlp_w2"],
        "even_w_in": inp["even_w_in"], "even_w_out": inp["even_w_out"],
        "odd_w_in": inp["odd_w_in"], "odd_w_out": inp["odd_w_out"], "glu_w": inp["ssm_glu_w"],
        "dlam": np.ascontiguousarray(np.broadcast_to(inp["diff_lambda"][None], (128, 2, 4, 64))),
        "sublnT": np.ascontiguousarray(inp["diff_subln"].T),
        "conv_wT": np.ascontiguousarray(inp["conv_w"].reshape(2, 31, 8, 128).transpose(3, 0, 2, 1)),
        "conv_bT": _fm(inp["conv_b"]), "conv_lnT": _fm(inp["conv_ln"]),
        "ssm_dT": _fm(inp["ssm_d"]), "glu_bT": _fm(inp["ssm_glu_b"]),
        "qk_normT": np.ascontiguousarray(inp["qk_norm"].transpose(2, 0, 1)),
        "ident": np.eye(128, dtype=f32), "ropeA": ropeA, "ropeD": ropeD, "permA": permA, "permD": permD,
        "tvec": np.ascontiguousarray(np.broadcast_to(np.arange(TT, dtype=f32)[None], (128, TT))),
    }

    def st_layout(a):
        o_ = a.reshape(2, 2, 32, 2, 64)
        return np.ascontiguousarray(o_.transpose(3, 4, 0, 1, 2).reshape(128, 2, 64))

    shared["a_reT"] = st_layout(inp["ssm_a_re"])
    shared["a_imT"] = st_layout(inp["ssm_a_im"])
    shared["ldtT"] = st_layout(np.broadcast_to(inp["ssm_log_dt"][..., None], (2, 2, 64, 64)))
    b = inp["ssm_b"]
    c = inp["ssm_c"]
    bz = np.zeros((2, 2, 32, 128, 2, 128), f32)
    cz = np.zeros((2, 2, 32, 128, 2, 128), f32)
    for jt in range(32):
        for gl in range(2):
            g = 2 * jt + gl
            c0 = (g % 8) * 16
            bz[:, :, jt, gl * 64:(gl + 1) * 64, :, c0:c0 + 16] = b[:, :, :, g].transpose(0, 1, 3, 2, 4)
            cz[:, :, jt, gl * 64:(gl + 1) * 64, :, c0:c0 + 16] = c[:, :, :, g].transpose(0, 1, 4, 2, 3)
    shared["ssm_bz"] = bz
    shared["ssm_cz"] = cz

    in_maps = []
    for core in range(NCORES):
        sidx = core % 2
        m = dict(shared)
        xs = inp["x_sample"][sidx]
        xp = inp["x_prompt"][core * 4:(core + 1) * 4].reshape(TT, 2048)
        m["xg"] = np.ascontiguousarray(np.stack([xs.T.reshape(16, 128, TT).transpose(1, 0, 2),
                                                 xp.T.reshape(16, 128, TT).transpose(1, 0, 2)]))
        m["cond"] = np.ascontiguousarray(np.stack([_fm(inp["c"][sidx]), _fm(inp["c_ctx"])], axis=-1))
        m["ca_kT"] = np.ascontiguousarray(inp["cache_a_k"][sidx].transpose(0, 1, 3, 2))
        m["ca_v"] = np.ascontiguousarray(inp["cache_a_v"][sidx])
        m["cd_kT"] = np.ascontiguousarray(inp["cache_d_k"][sidx].transpose(0, 1, 3, 2))
        m["cd_v"] = np.ascontiguousarray(inp["cache_d_v"][sidx])
        h0 = inp["state_c_ssm"][sidx]
        h0 = h0.reshape(2, 2, 2, 32, 2, 64).transpose(4, 5, 0, 2, 1, 3).reshape(128, 2, 2, 64)
        m["h0T"] = np.ascontiguousarray(h0)
        in_maps.append(m)
    res = run_bass_kernel_spmd(nc, in_maps, core_ids=list(range(NCORES)))
    R = res.results
    if os.environ.get("MK_DBG"):
        _CACHE["dbg"] = R[0]["dbg"]
    y_prompt = np.zeros((32, 256, 2048), f32)
    y_sample = np.zeros((2, 1024, 2048), f32)
    new_a_k = np.zeros((32, 2, 8, 256, 128), f32)
    new_a_v = np.zeros((32, 2, 8, 256, 128), f32)
    new_d_k = np.zeros((32, 2, 2, 256, 128), f32)
    new_d_v = np.zeros((32, 2, 2, 256, 128), f32)
    new_c = np.zeros((32, 2, 2, 2, 64, 64), f32)
    for core in range(NCORES):
        r = R[core]
        yg = r["yg"]
        ytm = yg.transpose(0, 3, 2, 1).reshape(2, TT, 2048)
        if core < 2:
            y_sample[core] = ytm[0]
        bs = slice(core * 4, (core + 1) * 4)
        y_prompt[bs] = ytm[1].reshape(4, 256, 2048)
        new_a_k[bs] = r["o_akT"].reshape(2, 8, 128, 4, 256).transpose(3, 0, 1, 4, 2)
        new_a_v[bs] = r["o_av"].reshape(2, 4, 256, 8, 128).transpose(1, 0, 3, 2, 4)
        new_d_k[bs] = r["o_dkT"].reshape(2, 2, 128, 4, 256).transpose(3, 0, 1, 4, 2)
        new_d_v[bs] = r["o_dv"].reshape(2, 4, 256, 2, 128).transpose(1, 0, 3, 2, 4)
        t = r["o_ssm"].reshape(2, 2, 64, 2, 2, 32, 4)
        new_c[bs] = t.transpose(6, 0, 3, 4, 5, 1, 2).reshape(4, 2, 2, 2, 64, 64)
    return (y_prompt, y_sample, new_a_k, new_a_v, new_d_k, new_d_v, new_c)
```

```python
import math
import os
from contextlib import ExitStack
import numpy as np
import concourse.bass as bass
import concourse.mybir as mybir
from concourse.bass_utils import run_bass_kernel_spmd

F32 = mybir.dt.float32
BF16 = mybir.dt.bfloat16
I32 = mybir.dt.int32
AF = mybir.ActivationFunctionType
ALU = mybir.AluOpType
AX = mybir.AxisListType

N_DMA_SEMS = 40
EPS = 1e-6
TT = 1024
NWR = 3


class Buf:
    __slots__ = ("ap", "keys")

    def __init__(self, ap, keys):
        self.ap = ap
        self.keys = keys


def _flat(items):
    out = []
    for it in items:
        if isinstance(it, Buf):
            out.extend(it.keys)
        elif isinstance(it, list):
            out.extend(_flat(it))
        else:
            out.append(it)
    return out


class Prog:
    def __init__(self, nc):
        self.nc = nc
        self.ops = []
        self.last_w = {}
        self.readers = {}

    def add(self, eng, fn, r=(), w=(), dma=False):
        reads = _flat(r)
        writes = _flat(w)
        idx = len(self.ops)
        deps = set()
        lw, rd = self.last_w, self.readers
        for k in reads:
            x = lw.get(k)
            if x is not None:
                deps.add(x)
        for k in writes:
            x = lw.get(k)
            if x is not None:
                deps.add(x)
            rr = rd.get(k)
            if rr:
                deps.update(rr)
        for k in reads:
            rd.setdefault(k, []).append(idx)
        for k in writes:
            lw[k] = idx
            rd[k] = []
        self.ops.append([eng, fn, deps, dma, False, None])
        return idx

    def pe(self, fn, r=(), w=()):
        return self.add('pe', fn, r, w)

    def act(self, fn, r=(), w=()):
        return self.add('act', fn, r, w)

    def dve(self, fn, r=(), w=()):
        return self.add('dve', fn, r, w)

    def dma(self, q, fn, r=(), w=()):
        return self.add(q, fn, r, w, dma=True)

    def emit(self):
        nc = self.nc
        ops = self.ops
        for op in ops:
            best = {}
            keep = set()
            for d in op[2]:
                dop = ops[d]
                if dop[3]:
                    keep.add(d)
                elif best.get(dop[0], -1) < d:
                    best[dop[0]] = d
            keep.update(best.values())
            op[2] = keep
        for op in ops:
            eng, fn, deps, dma, _, _ = op
            for d in deps:
                dop = ops[d]
                if dop[0] == 'pe' and eng == 'pe' and not dop[3] and not dma:
                    continue
                dop[4] = True
        with ExitStack() as st:
            cnt = {e: 0 for e in ('pe', 'act', 'dve', 'pool')}
            for op in ops:
                if (not op[3]) and op[4]:
                    cnt[op[0]] += 1
            CH = {e: max(3000, -(-cnt[e] // 13)) for e in cnt}
            csem = {e: [st.enter_context(nc.semaphore('s_%s_%d' % (e, k))) for k in range(max(1, -(-cnt[e] // CH[e])))]
                    for e in cnt}
            dsems = [st.enter_context(nc.semaphore('d%d' % i)) for i in range(N_DMA_SEMS)]
            print("signal counts", cnt, "epoch sizes", CH, "n_ops", len(ops))
            ccount = {e: 0 for e in cnt}
            dcount = [0] * N_DMA_SEMS
            dnext = 0
            for op in ops:
                eng, fn, deps, dma, sig, _ = op
                if dma:
                    s = dnext
                    dnext = (dnext + 1) % N_DMA_SEMS
                    prev = dcount[s]
                    dcount[s] += 16
                    op[5] = ('d', s, dcount[s], prev)
                elif sig:
                    n = ccount[eng]
                    ccount[eng] += 1
                    op[5] = ('c', eng, n // CH[eng], n % CH[eng] + 1)
            streams = {e: [] for e in ('pe', 'act', 'dve', 'pool', 'sp')}
            for i, op in enumerate(ops):
                streams[op[0]].append(i)
            block = st.enter_context(nc.Block())

            def run_stream(ename, eng):
                waited = {}
                waited_c = {}

                def wait(sem, key, val):
                    if waited.get(key, 0) >= val:
                        return
                    waited[key] = val
                    eng.wait_ge(sem, val)

                def wait_c(pe_, ep, val):
                    cur = waited_c.get(pe_, (-1, 0))
                    if (ep, val) <= cur:
                        return
                    waited_c[pe_] = (ep, val)
                    eng.wait_ge(csem[pe_][ep], val)

                for i in streams[ename]:
                    e, fn, deps, dma, sig, tok = ops[i]
                    for d in sorted(deps):
                        t = ops[d][5]
                        if t is None:
                            continue
                        if t[0] == 'c':
                            if t[1] == 'pe' and ename == 'pe' and not dma:
                                continue
                            wait_c(t[1], t[2], t[3])
                        else:
                            wait(dsems[t[1]], ('d', t[1]), t[2])
                    if dma:
                        _, s, val, prev = tok
                        if prev:
                            wait(dsems[s], ('d', s), prev)
                        fn(eng).then_inc(dsems[s], 16)
                    else:
                        ins = fn(eng)
                        if sig:
                            ins.then_inc(csem[e][tok[2]], 1)
                if ename == 'sp':
                    for s in range(N_DMA_SEMS):
                        if dcount[s]:
                            wait(dsems[s], ('d', s), dcount[s])

            @block.tensor
            def _(eng):
                run_stream('pe', eng)

            @block.scalar
            def _(eng):
                run_stream('act', eng)

            @block.vector
            def _(eng):
                run_stream('dve', eng)

            @block.gpsimd
            def _(eng):
                run_stream('pool', eng)

            @block.sync
            def _(eng):
                run_stream('sp', eng)


IN_SPECS = [
    ("xg", [2, 128, 16, TT]), ("cond", [128, 16, 2]),
    ("ada_w", [4, 2048, 12288]), ("ada_bT", [128, 4, 96]),
    ("norm_gT", [128, 4, 2, 16]), ("final_normT", [128, 16]),
    ("mlp_w1", [4, 2048, 8192]), ("mlp_w2", [4, 8192, 2048]),
    ("even_w_in", [2, 2048, 5120]), ("even_w_out", [2, 2048, 2048]),
    ("odd_w_in", [2, 2048, 2560]), ("odd_w_out", [2, 2048, 2048]),
    ("glu_w", [2, 1024, 1024]),
    ("dlam", [128, 2, 4, 64]), ("sublnT", [128, 2]),
    ("conv_wT", [128, 2, 8, 31]), ("conv_bT", [128, 2, 8]), ("conv_lnT", [128, 2, 2, 8]),
    ("a_reT", [128, 2, 64]), ("a_imT", [128, 2, 64]), ("ldtT", [128, 2, 64]),
    ("ssm_bz", [2, 2, 32, 128, 2, 128]), ("ssm_cz", [2, 2, 32, 128, 2, 128]),
    ("ssm_dT", [128, 2, 8]), ("glu_bT", [128, 2, 8]), ("qk_normT", [128, 2, 2]),
    ("h0T", [128, 2, 2, 64]),
    ("ca_kT", [2, 8, 128, 512]), ("ca_v", [2, 8, 512, 128]),
    ("cd_kT", [2, 2, 128, 512]), ("cd_v", [2, 2, 512, 128]),
    ("ident", [128, 128]), ("ropeA", [2, 128, TT]), ("ropeD", [2, 128, TT]),
    ("permA", [128, 128]), ("permD", [128, 128]), ("tvec", [128, TT]),
]
OUT_SPECS = [
    ("yg", [2, 128, 16, TT]),
    ("o_akT", [2, 8, 128, TT]), ("o_av", [2, TT, 1024]),
    ("o_dkT", [2, 2, 128, TT]), ("o_dv", [2, TT, 256]),
    ("o_ssm", [2, 128, 2, 2, 32, 4]),
] + ([("dbg", [6, 128, 1024])] if os.environ.get("MK_DBG") else [])


def build():
    nc = bass.Bass("TRN2", target_bir_lowering=False)
    D = {}
    for name, shp in IN_SPECS:
        D[name] = nc.dram_tensor(name, list(shp), F32, kind="ExternalInput").ap()
    for name, shp in OUT_SPECS:
        D[name] = nc.dram_tensor(name, list(shp), F32, kind="ExternalOutput").ap()
    P = Prog(nc)
    with ExitStack() as st:
        def sb(name, shape, dt=F32):
            return st.enter_context(nc.sbuf_tensor("sb_" + name, list(shape), dt))

        X = sb("X", [128, 16, TT])
        H = sb("H", [128, 16, TT], BF16)
        MIX = sb("MIX", [128, 16, TT], BF16)
        WR = sb("WR", [128, NWR, 4096], BF16)
        SCRW = 9216
        SCR = sb("SCR", [128, SCRW])
        ps = [st.enter_context(nc.psum_tensor("ps%d" % i, [128, 512], F32)) for i in range(8)]
        PSK = [('ps', i) for i in range(8)]

        ident = sb("ident", [128, 128]); ones_f = sb("ones_f", [128, 128]); ones_b = sb("ones_b", [128, 128], BF16)
        permA_f = sb("permA_f", [128, 128]); permD_f = sb("permD_f", [128, 128])
        permA = sb("permA", [128, 128], BF16); permD = sb("permD", [128, 128], BF16)
        MOD = sb("MOD", [128, 4, 96, 2]); ADAB = sb("ADAB", [128, 4, 96]); NORMG = sb("NORMG", [128, 4, 2, 16])
        FNORM = sb("FNORM", [128, 16]); CONDF = sb("CONDF", [128, 16, 2]); SC = sb("SC", [128, 16, 2], BF16)
        GS = sb("GS", [128, 16])
        DLAM = sb("DLAM", [128, 2, 4, 64]); SUBLN = sb("SUBLN", [128, 2]); LT = sb("LT", [128, 64]); LS = sb("LS", [128, 4])
        CONVW = sb("CONVW", [128, 2, 8, 31]); CONVB = sb("CONVB", [128, 2, 8]); CONVLN = sb("CONVLN", [128, 2, 2, 8])
        SSMD = sb("SSMD", [128, 2, 8]); GLUB = sb("GLUB", [128, 2, 8]); QKN = sb("QKN", [128, 2, 2])
        ARE = sb("ARE", [128, 2, 64]); AIM = sb("AIM", [128, 2, 64]); LDT = sb("LDT", [128, 2, 64]); H0 = sb("H0", [128, 2, 2, 64])
        SP = sb("SP", [128, 12, 64]); SPI = sb("SPI", [128, 64], I32)

        def cload(t, src, key):
            P.dma('sp', lambda e: e.dma_start(out=t[:], in_=src), w=[key])

        for t, n in [(ident, "ident"), (permA_f, "permA"), (permD_f, "permD"), (ADAB, "ada_bT"), (NORMG, "norm_gT"),
                     (FNORM, "final_normT"), (CONDF, "cond"), (DLAM, "dlam"), (SUBLN, "sublnT"), (CONVW, "conv_wT"),
                     (CONVB, "conv_bT"), (CONVLN, "conv_lnT"), (SSMD, "ssm_dT"), (GLUB, "glu_bT"), (QKN, "qk_normT"),
                     (ARE, "a_reT"), (AIM, "a_imT"), (LDT, "ldtT"), (H0, "h0T")]:
            cload(t, D[n], 'c_' + n)
        CK = ['c_' + n for n in ["ident", "permA", "permD", "ada_bT", "norm_gT", "final_normT", "cond", "dlam", "sublnT",
                                 "conv_wT", "conv_bT", "conv_lnT", "ssm_dT", "glu_bT", "qk_normT", "a_reT", "a_imT", "ldtT", "h0T"]]
        P.dve(lambda e: e.memset(ones_f[:], 1.0), w=['ones'])
        P.dve(lambda e: e.memset(ones_b[:], 1.0), w=['ones'])
        P.dve(lambda e: e.tensor_copy(out=permA[:], in_=permA_f[:]), r=['c_permA'], w=['perm'])
        P.dve(lambda e: e.tensor_copy(out=permD[:], in_=permD_f[:]), r=['c_permD'], w=['perm'])
        CONST = ['ones', 'perm'] + CK
        if SKIP - {''}:
            P.dve(lambda e: e.memset(MIX[:], 0.0), w=[('MIX', c) for c in range(16)])

        def scr(off, n, dt=F32):
            ap = SCR[:, off:off + n]
            if dt != F32:
                ap = ap.bitcast(dt)
            return Buf(ap, [('scr', g) for g in range(off // 256, (off + n + 255) // 256)])

        MIXflat = MIX[:].rearrange("p a b -> p (a b)").bitcast(F32)

        def mixf(off, n, dt=F32):
            ap = MIXflat[:, off:off + n]
            if dt != F32:
                ap = ap.bitcast(dt)
            return Buf(ap, [('MIX', c) for c in range(off // 512, (off + n + 511) // 512)])

        def sub(b, ap):
            return Buf(ap, b.keys)

        wr_n = [0]

        def wload(pieces):
            s = wr_n[0] % NWR
            wr_n[0] += 1
            n = len(pieces)
            for i, (dst_fn, src) in enumerate(pieces):
                keys = [('wr', s, i)] if n == 2 else [('wr', s, 0), ('wr', s, 1)]
                P.dma('pool', lambda e, d=dst_fn(s), src=src: e.dma_start(out=d, in_=src), w=keys)
            return s

        def wrk(s):
            return [('wr', s, 0), ('wr', s, 1)]

        def wview(s, k, c):
            return WR[:, s, 0:k * c].rearrange("p (k c) -> p k c", k=k)

        def hv_sl(hv):
            return slice(hv * 512, (hv + 1) * 512)

        bank_rr = [0]

        def nbank(lo=0, n=8):
            b = lo + bank_rr[0] % n
            bank_rr[0] += 1
            return b

        P.act(lambda e: e.activation(out=SC[:], in_=CONDF[:], func=AF.Silu), r=['c_cond'], w=['SC'])
        def ada_blocks(l, bank_fn):
            wv = D["ada_w"][l].rearrange("(kc p) n -> p kc n", p=128)
            for blk in range(48):
                s = wload([(lambda s: wview(s, 16, 256), wv[:, :, blk * 256:(blk + 1) * 256])])
                pb = bank_fn()
                for mi in range(2):
                    for kc in range(16):
                        P.pe(lambda e, s=s, mi=mi, kc=kc, pb=pb: e.matmul(
                            ps[pb][:, mi * 2:mi * 2 + 2], lhsT=wview(s, 16, 256)[:, kc, mi * 128:(mi + 1) * 128],
                            rhs=SC[:, kc, :], start=(kc == 0), stop=(kc == 15)),
                            r=wrk(s) + ['SC'], w=[PSK[pb]])
                for cnd in range(2):
                    P.dve(lambda e, l=l, cnd=cnd, pb=pb, blk=blk: e.tensor_add(
                        out=MOD[:, l, blk * 2:blk * 2 + 2, cnd], in0=ps[pb][:, 0:4].rearrange("p (m c) -> p m c", c=2)[:, :, cnd],
                        in1=ADAB[:, l, blk * 2:blk * 2 + 2]), r=[PSK[pb], 'c_ada_bT'], w=[('MOD', l)])
                yield

        def ada_steps(gen, n):
            if gen is None:
                return
            for _ in range(n):
                if next(gen, 'done') == 'done':
                    break

        for _ in ada_blocks(0, lambda: nbank(0, 2)):
            pass
        NEXT_ADA = [None]

        def XK(kc):
            return ('X', kc)

        def HK(kc):
            return ('H', kc)

        ALLH = [HK(k) for k in range(16)]

        def phase_norm(l, which, cnd, final=False, grp=0):
            SQ = [scr(0, 1024), scr(1024, 1024)]
            RSTD = scr(2048, 1024)
            TMP = [scr(3072, 1024), scr(4096, 1024)]
            if not final:
                P.dve(lambda e: e.tensor_scalar_add(out=GS[:], in0=MOD[:, l, (3 * which + 1) * 16:(3 * which + 2) * 16, cnd],
                                                    scalar1=1.0), r=[('MOD', l)], w=['GS'])
                P.dve(lambda e: e.tensor_mul(out=GS[:], in0=GS[:], in1=NORMG[:, l, which, :]), r=['GS', 'c_norm_gT'], w=['GS'])
            for kc in range(16):
                sq = SQ[kc % 2]
                P.act(lambda e, kc=kc, sq=sq: e.activation(out=sq.ap, in_=X[:, kc, :], func=AF.Square), r=[XK(kc)], w=[sq])
                for hv in range(2):
                    P.pe(lambda e, kc=kc, sq=sq, hv=hv: e.matmul(ps[hv][:], lhsT=ones_f[:], rhs=sq.ap[:, hv_sl(hv)],
                                                                 start=(kc == 0), stop=(kc == 15)), r=[sq, 'ones'], w=[PSK[hv]])
            for hv in range(2):
                P.act(lambda e, hv=hv: e.activation(out=RSTD.ap[:, hv_sl(hv)], in_=ps[hv][:], func=AF.Sqrt,
                                                    scale=1.0 / 2048, bias=EPS), r=[PSK[hv]], w=[RSTD])
            P.dve(lambda e: e.reciprocal(out=RSTD.ap, in_=RSTD.ap), r=[RSTD], w=[RSTD])
            for kc in range(16):
                tmp = TMP[kc % 2]
                P.dve(lambda e, kc=kc, tmp=tmp: e.tensor_mul(out=tmp.ap, in0=X[:, kc, :], in1=RSTD.ap), r=[XK(kc), RSTD], w=[tmp])
                if final:
                    P.act(lambda e, kc=kc, tmp=tmp: e.activation(out=tmp.ap, in_=tmp.ap, func=AF.Identity,
                                                                 scale=FNORM[:, kc:kc + 1]), r=[tmp, 'c_final_normT'], w=[tmp])
                    P.dma('sp', lambda e, kc=kc, tmp=tmp: e.dma_start(out=D["yg"][grp, :, kc, :], in_=tmp.ap), r=[tmp])
                else:
                    sh = MOD[:, l, (3 * which) * 16 + kc:(3 * which) * 16 + kc + 1, cnd]
                    P.act(lambda e, kc=kc, tmp=tmp, sh=sh: e.activation(out=H[:, kc, :], in_=tmp.ap, func=AF.Identity,
                                                                        scale=GS[:, kc:kc + 1], bias=sh),
                          r=[tmp, 'GS', ('MOD', l)], w=[HK(kc)])

        def phase_wout(wdram, l, cnd):
            wv = wdram.rearrange("(kc p) n -> p kc n", p=128)
            for blk in range(8):
                s = wload([(lambda s: wview(s, 16, 256), wv[:, :, blk * 256:(blk + 1) * 256])])
                for dci in range(2):
                    dc = blk * 2 + dci
                    gate = MOD[:, l, 32 + dc:32 + dc + 1, cnd]
                    for hv in range(2):
                        b = nbank()
                        for kc in range(16):
                            P.pe(lambda e, s=s, dci=dci, hv=hv, kc=kc, b=b: e.matmul(
                                ps[b][:], lhsT=wview(s, 16, 256)[:, kc, dci * 128:(dci + 1) * 128], rhs=MIX[:, kc, hv_sl(hv)],
                                start=(kc == 0), stop=(kc == 15)), r=wrk(s) + [('MIX', kc)], w=[PSK[b]])
                        P.dve(lambda e, dc=dc, hv=hv, b=b, gate=gate: e.scalar_tensor_tensor(
                            out=X[:, dc, hv_sl(hv)], in0=ps[b][:], scalar=gate, in1=X[:, dc, hv_sl(hv)],
                            op0=ALU.mult, op1=ALU.add), r=[PSK[b], XK(dc), ('MOD', l)], w=[XK(dc)])

        def phase_mlp(l, cnd):
            w1v = D["mlp_w1"][l].rearrange("(kc p) n -> p kc n", p=128)
            w2v = D["mlp_w2"][l].rearrange("(kc p) n -> p kc n", p=128)
            RL = [scr(0, 512), scr(512, 512)]
            rl_n = 0
            for hb in range(16):
                hid = mixf((hb % 2) * 2048, 2048, BF16)
                hidv = hid.ap.rearrange("p (k t) -> p k t", k=4)
                for sb_ in range(2):
                    c0 = hb * 512 + sb_ * 256
                    s = wload([(lambda s: wview(s, 16, 256), w1v[:, :, c0:c0 + 256])])
                    for mi in range(2):
                        for hv in range(2):
                            b = nbank()
                            for kc in range(16):
                                P.pe(lambda e, s=s, mi=mi, hv=hv, kc=kc, b=b: e.matmul(
                                    ps[b][:], lhsT=wview(s, 16, 256)[:, kc, mi * 128:(mi + 1) * 128], rhs=H[:, kc, hv_sl(hv)],
                                    start=(kc == 0), stop=(kc == 15)), r=wrk(s) + [HK(kc)], w=[PSK[b]])
                            rl = RL[rl_n % 2]
                            rl_n += 1
                            P.act(lambda e, b=b, rl=rl: e.activation(out=rl.ap, in_=ps[b][:], func=AF.Relu), r=[PSK[b]], w=[rl])
                            P.act(lambda e, rl=rl, k=sb_ * 2 + mi, hv=hv, hidv=hidv: e.activation(
                                out=hidv[:, k, hv_sl(hv)], in_=rl.ap, func=AF.Square), r=[rl], w=[hid])
                for half in range(2):
                    s = wload([(lambda s: wview(s, 4, 1024), w2v[:, hb * 4:(hb + 1) * 4, half * 1024:(half + 1) * 1024])])
                    for dci in range(8):
                        dc = half * 8 + dci
                        gate = MOD[:, l, 80 + dc:80 + dc + 1, cnd]
                        for hv in range(2):
                            b = nbank()
                            for k4 in range(4):
                                P.pe(lambda e, s=s, dci=dci, hv=hv, k4=k4, b=b, hidv=hidv: e.matmul(
                                    ps[b][:], lhsT=wview(s, 4, 1024)[:, k4, dci * 128:(dci + 1) * 128], rhs=hidv[:, k4, hv_sl(hv)],
                                    start=(k4 == 0), stop=(k4 == 3)), r=wrk(s) + [hid], w=[PSK[b]])
                            P.dve(lambda e, dc=dc, hv=hv, b=b, gate=gate: e.scalar_tensor_tensor(
                                out=X[:, dc, hv_sl(hv)], in0=ps[b][:], scalar=gate, in1=X[:, dc, hv_sl(hv)],
                                op0=ALU.mult, op1=ALU.add), r=[PSK[b], XK(dc), ('MOD', l)], w=[XK(dc)])

        def proj_fm(s, col_off, banks):
            for hv in range(2):
                b = banks[hv]
                for kc in range(16):
                    P.pe(lambda e, s=s, hv=hv, kc=kc, b=b, col_off=col_off: e.matmul(
                        ps[b][:], lhsT=wview(s, 16, 256)[:, kc, col_off:col_off + 128], rhs=H[:, kc, hv_sl(hv)],
                        start=(kc == 0), stop=(kc == 15)), r=wrk(s) + [HK(kc)], w=[PSK[b]])

        def attn_core(streams, KT, Vfn, scale, n_seq, L, ctx, ET, post):
            QB = min(L, 512)
            for sq_ in range(n_seq):
                kts = [(sq_ * L + j * 128, (sq_ * L) // 128 + j) for j in range(L // 128)]
                if ctx:
                    kts += [(1024 + j * 128, 8 + j) for j in range(4)]
                for qb in range(L // QB):
                    q0 = sq_ * L + qb * QB
                    nk = len(kts)
                    for idx, (kcol, kt) in enumerate(kts if 'ea_core' not in SKIP else []):
                        for i, (p0, p1, QT) in enumerate(streams):
                            sbk = i * 2 + (idx % 2)
                            E = ET[sbk]
                            P.pe(lambda e, sbk=sbk, p0=p0, p1=p1, kcol=kcol, QT=QT, q0=q0: e.matmul(
                                ps[sbk][:, :QB], lhsT=KT.ap[p0:p1, kcol:kcol + 128], rhs=QT.ap[p0:p1, q0:q0 + QB],
                                start=True, stop=True), r=[KT, QT], w=[PSK[sbk]])
                            P.act(lambda e, sbk=sbk, E=E: e.activation(out=E.ap[:, :QB], in_=ps[sbk][:, :QB], func=AF.Exp,
                                                                       scale=scale), r=[PSK[sbk]], w=[E])
                            vap, vk = Vfn(i, kt)
                            P.pe(lambda e, i=i, E=E, vap=vap, idx=idx: e.matmul(
                                ps[4 + 2 * i][:, :QB], lhsT=vap, rhs=E.ap[:, :QB], start=(idx == 0), stop=(idx == nk - 1)),
                                r=[E] + vk, w=[PSK[4 + 2 * i]])
                            P.pe(lambda e, i=i, E=E, idx=idx: e.matmul(
                                ps[5 + 2 * i][:, :QB], lhsT=ones_b[:], rhs=E.ap[:, :QB], start=(idx == 0), stop=(idx == nk - 1)),
                                r=[E, 'ones'], w=[PSK[5 + 2 * i]])
                    if 'ea_post' not in SKIP:
                        post(q0, QB)

        def rope_combine(b, hv, dst_ap, dstk, src_f32, QBb, T1, T2, COS, SIN, perm, sb7):
            srcap = src_f32.ap if src_f32 is not None else ps[b][:]
            srck = [src_f32] if src_f32 is not None else [PSK[b]]
            P.act(lambda e: e.copy(out=QBb.ap, in_=srcap), r=srck, w=[QBb])
            P.pe(lambda e: e.matmul(ps[sb7][:], lhsT=perm[:], rhs=QBb.ap, start=True, stop=True), r=[QBb, 'perm'], w=[PSK[sb7]])
            P.dve(lambda e: e.tensor_mul(out=T2.ap, in0=ps[sb7][:], in1=SIN.ap[:, hv_sl(hv)]), r=[PSK[sb7], SIN], w=[T2])
            P.dve(lambda e: e.tensor_mul(out=T1.ap, in0=srcap, in1=COS.ap[:, hv_sl(hv)]), r=srck + [COS], w=[T1])
            P.dve(lambda e: e.tensor_add(out=dst_ap, in0=T1.ap, in1=T2.ap), r=[T1, T2], w=dstk)

        def even_mixer(e_, l, grp, n_seq, L, ctx):
            win = D["even_w_in"][e_].rearrange("(kc p) n -> p kc n", p=128)
            lam_init = 0.8 - 0.6 * math.exp(-0.3 * l)
            for i in (range(2) if 'ea_lam' not in SKIP else []):
                P.dve(lambda e, i=i: e.tensor_mul(out=LT[:], in0=DLAM[:, e_, 2 * i, :], in1=DLAM[:, e_, 2 * i + 1, :]), r=['c_dlam'], w=['LT'])
                P.dve(lambda e, i=i: e.reduce_sum(out=LS[:, i:i + 1], in_=LT[:], axis=AX.X), r=['LT'], w=['LS'])
            P.act(lambda e: e.activation(out=LS[:, 0:2], in_=LS[:, 0:2], func=AF.Exp), r=['LS'], w=['LS'])
            P.dve(lambda e: e.tensor_sub(out=LS[:, 2:3], in0=LS[:, 1:2], in1=LS[:, 0:1]), r=['LS'], w=['LS'])
            P.dve(lambda e: e.tensor_scalar_add(out=LS[:, 2:3], in0=LS[:, 2:3], scalar1=-lam_init), r=['LS'], w=['LS'])
            P.dve(lambda e: e.tensor_scalar_mul(out=LS[:, 3:4], in0=SUBLN[:, e_:e_ + 1], scalar1=1.0 - lam_init), r=['c_sublnT'], w=['LS'])

            def conv():
                CP = [mixf(c * 1024, 1024) if c < 4 else scr((c - 4) * 1024, 1024) for c in range(8)]
                hw = n_seq * (L + 30)
                HHP = scr(4096, hw)
                HHPv = HHP.ap.rearrange("p (s l) -> p s l", s=n_seq)
                SIG = scr(5376, 1024)
                MEAN = scr(6400, 1024)
                R2 = scr(7424, 1024)
                P.dve(lambda e: e.memset(HHP.ap, 0.0), w=[HHP])
                for c in range(8):
                    ada_steps(NEXT_ADA[0], 6)
                    s = wload([(lambda s: wview(s, 16, 256)[:, :, 0:128], win[:, :, 3072 + c * 128:3072 + (c + 1) * 128]),
                               (lambda s: wview(s, 16, 256)[:, :, 128:256], win[:, :, 4096 + c * 128:4096 + (c + 1) * 128])])
                    proj_fm(s, 0, [0, 1])
                    proj_fm(s, 128, [2, 3])
                    for hv in range(2):
                        P.act(lambda e, hv=hv: e.activation(out=SIG.ap[:, hv_sl(hv)], in_=ps[2 + hv][:], func=AF.Sigmoid),
                              r=[PSK[2 + hv]], w=[SIG])
                        if n_seq == 1:
                            dst = HHPv[:, 0, 15 + hv * 512:15 + hv * 512 + 512]
                            a_in = ps[hv][:]
                            s_in = SIG.ap[:, hv_sl(hv)]
                        else:
                            spp = 512 // L
                            dst = HHPv[:, hv * spp:(hv + 1) * spp, 15:15 + L]
                            a_in = ps[hv][:].rearrange("p (s l) -> p s l", s=spp)
                            s_in = SIG.ap[:, hv_sl(hv)].rearrange("p (s l) -> p s l", s=spp)
                        P.dve(lambda e, dst=dst, a_in=a_in, s_in=s_in: e.tensor_tensor(out=dst, in0=a_in, in1=s_in, op=ALU.mult),
                              r=[PSK[hv], SIG], w=[HHP])
                    acc = CP[c]
                    accv = acc.ap.rearrange("p (s l) -> p s l", s=n_seq)
                    P.dve(lambda e, c=c, accv=accv: e.tensor_scalar(out=accv, in0=HHPv[:, :, 0:L], scalar1=CONVW[:, e_, c, 0:1],
                                                                    scalar2=CONVB[:, e_, c:c + 1], op0=ALU.mult, op1=ALU.add),
                          r=[HHP, 'c_conv_wT', 'c_conv_bT'], w=[acc])
                    for k in range(1, 31):
                        P.dve(lambda e, c=c, k=k, accv=accv: e.scalar_tensor_tensor(
                            out=accv, in0=HHPv[:, :, k:k + L], scalar=CONVW[:, e_, c, k:k + 1], in1=accv, op0=ALU.mult, op1=ALU.add),
                            r=[HHP, acc, 'c_conv_wT'], w=[acc])
                for c in range(8):
                    P.act(lambda e, c=c: e.activation(out=SIG.ap, in_=CP[c].ap, func=AF.Square), r=[CP[c]], w=[SIG])
                    for hv in range(2):
                        P.pe(lambda e, c=c, hv=hv: e.matmul(ps[4 + hv][:], lhsT=ones_f[:], rhs=CP[c].ap[:, hv_sl(hv)],
                                                            start=(c == 0), stop=(c == 7)), r=[CP[c], 'ones'], w=[PSK[4 + hv]])
                        P.pe(lambda e, c=c, hv=hv: e.matmul(ps[6 + hv][:], lhsT=ones_f[:], rhs=SIG.ap[:, hv_sl(hv)],
                                                            start=(c == 0), stop=(c == 7)), r=[SIG, 'ones'], w=[PSK[6 + hv]])
                for hv in range(2):
                    P.act(lambda e, hv=hv: e.activation(out=MEAN.ap[:, hv_sl(hv)], in_=ps[4 + hv][:], func=AF.Identity, scale=1.0 / 1024),
                          r=[PSK[4 + hv]], w=[MEAN])
                    P.dve(lambda e, hv=hv: e.tensor_mul(out=R2.ap[:, hv_sl(hv)], in0=MEAN.ap[:, hv_sl(hv)], in1=MEAN.ap[:, hv_sl(hv)]),
                          r=[MEAN], w=[R2])
                    P.dve(lambda e, hv=hv: e.scalar_tensor_tensor(out=R2.ap[:, hv_sl(hv)], in0=ps[6 + hv][:], scalar=1.0 / 1024,
                                                                  in1=R2.ap[:, hv_sl(hv)], op0=ALU.mult, op1=ALU.subtract),
                          r=[PSK[6 + hv], R2], w=[R2])
                P.act(lambda e: e.activation(out=R2.ap, in_=R2.ap, func=AF.Sqrt, bias=EPS), r=[R2], w=[R2])
                P.dve(lambda e: e.reciprocal(out=R2.ap, in_=R2.ap), r=[R2], w=[R2])
                if os.environ.get("MK_DBG") == "conv" and grp == 0 and l == 0:
                    for i_, bb in enumerate([CP[0], CP[4], MEAN, R2, HHP]):
                        n_ = min(1024, bb.ap.shape[1])
                        P.dma('sp', lambda e, i_=i_, bb=bb, n_=n_: e.dma_start(out=D["dbg"][i_, :, 0:n_], in_=bb.ap[:, 0:n_]), r=[bb])
                for c in range(8):
                    P.dve(lambda e, c=c: e.tensor_sub(out=SIG.ap, in0=CP[c].ap, in1=MEAN.ap), r=[CP[c], MEAN], w=[SIG])
                    P.dve(lambda e: e.tensor_mul(out=SIG.ap, in0=SIG.ap, in1=R2.ap), r=[SIG, R2], w=[SIG])
                    P.act(lambda e, c=c: e.activation(out=MIX[:, 8 + c, :], in_=SIG.ap, func=AF.Silu, scale=CONVLN[:, e_, 0, c:c + 1],
                                                      bias=CONVLN[:, e_, 1, c:c + 1]), r=[SIG, 'c_conv_lnT'], w=[('MIX', 8 + c)])

            if 'conv' not in SKIP:
                conv()
            def eattn():
                KT = scr(0, 768, BF16)
                V2 = scr(768, 1536, BF16)
                V2v = V2.ap.rearrange("p (t c) -> p t c", t=12)
                QT = scr(2304, 512, BF16)
                QTb = scr(8192, 512, BF16)
                ET = [scr(2816 + i * 256, 256, BF16) for i in range(4)]
                T1 = scr(3840, 512); T2 = scr(4352, 512); QBb = scr(4864, 256, BF16)
                STG = [scr(5120, 512), scr(5632, 512)]
                COS = scr(6144, 1024); SIN = scr(7168, 1024)
                stg_n = [0]
                if ctx:
                    P.dma('sp', lambda e: e.dma_start(out=COS.ap, in_=D["ropeA"][0]), w=[COS])
                    P.dma('sp', lambda e: e.dma_start(out=SIN.ap, in_=D["ropeA"][1]), w=[SIN])
                for hp in range(4):
                    s = wload([(lambda s: wview(s, 16, 256), win[:, :, 2048 + hp * 256:2048 + (hp + 1) * 256])])
                    for tt in (range(8) if 'ea_v' not in SKIP else []):
                        b = nbank(0, 4)
                        for kc in range(16):
                            P.pe(lambda e, s=s, tt=tt, kc=kc, b=b: e.matmul(
                                ps[b][:, 0:256], lhsT=H[:, kc, tt * 128:(tt + 1) * 128], rhs=wview(s, 16, 256)[:, kc, :],
                                start=(kc == 0), stop=(kc == 15)), r=wrk(s) + [HK(kc)], w=[PSK[b]])
                        P.act(lambda e, tt=tt, b=b: e.copy(out=V2v[:, tt, :], in_=ps[b][:, 0:256]), r=[PSK[b]], w=[V2])
                        if not ctx and 'ea_oav' not in SKIP:
                            sg = STG[stg_n[0] % 2]; stg_n[0] += 1
                            P.act(lambda e, b=b, sg=sg: e.copy(out=sg.ap[:, 0:256], in_=ps[b][:, 0:256]), r=[PSK[b]], w=[sg])
                            P.dma('sp', lambda e, tt=tt, sg=sg, hp=hp: e.dma_start(
                                out=D["o_av"][e_, tt * 128:(tt + 1) * 128, hp * 256:(hp + 1) * 256], in_=sg.ap[:, 0:256]), r=[sg])
                    if ctx:
                        for hh in range(2):
                            P.dma('pool', lambda e, hh=hh, hp=hp: e.dma_start(
                                out=V2v[:, 8:12, hh * 128:(hh + 1) * 128],
                                in_=D["ca_v"][e_, 2 * hp + hh].rearrange("(j p) d -> p j d", p=128)), w=[V2])
                    for hh in range(2):
                        h = 2 * hp + hh
                        s = wload([(lambda s: wview(s, 16, 256)[:, :, 0:128], win[:, :, h * 128:(h + 1) * 128]),
                                   (lambda s: wview(s, 16, 256)[:, :, 128:256], win[:, :, 1024 + h * 128:1024 + (h + 1) * 128])])
                        proj_fm(s, 0, [0, 1])
                        proj_fm(s, 128, [2, 3])
                        for hv in (range(2) if 'ea_qk' not in SKIP else []):
                            if ctx:
                                rope_combine(hv, hv, QT.ap[:, hv_sl(hv)], [QT], None, QBb, T1, T2, COS, SIN, permA, 7)
                                rope_combine(2 + hv, hv, KT.ap[:, hv_sl(hv)], [KT], None, QBb, T1, T2, COS, SIN, permA, 7)
                            else:
                                P.act(lambda e, hv=hv: e.copy(out=QT.ap[:, hv_sl(hv)], in_=ps[hv][:]), r=[PSK[hv]], w=[QT])
                                P.act(lambda e, hv=hv: e.copy(out=KT.ap[:, hv_sl(hv)], in_=ps[2 + hv][:]), r=[PSK[2 + hv]], w=[KT])
                                if 'ea_oak' not in SKIP:
                                    sg = STG[stg_n[0] % 2]; stg_n[0] += 1
                                    P.act(lambda e, hv=hv, sg=sg: e.copy(out=sg.ap, in_=ps[2 + hv][:]), r=[PSK[2 + hv]], w=[sg])
                                    P.dma('sp', lambda e, hv=hv, sg=sg, h=h: e.dma_start(out=D["o_akT"][e_, h, :, hv_sl(hv)], in_=sg.ap), r=[sg])
                        if ctx:
                            P.dma('pool', lambda e, h=h: e.dma_start(out=KT.ap[:, 1024:1536], in_=D["ca_kT"][e_, h]), w=[KT])

                        def post(q0, QB, h=h):
                            P.dve(lambda e: e.reciprocal(out=T1.ap[:, :QB], in_=ps[5][:, :QB]), r=[PSK[5]], w=[T1])
                            P.dve(lambda e: e.tensor_mul(out=T1.ap[:, :QB], in0=ps[4][:, :QB], in1=T1.ap[:, :QB]), r=[PSK[4], T1], w=[T1])
                            P.dve(lambda e: e.reciprocal(out=T2.ap[:, :QB], in_=ps[7][:, :QB]), r=[PSK[7]], w=[T2])
                            P.dve(lambda e: e.tensor_mul(out=T2.ap[:, :QB], in0=ps[6][:, :QB], in1=T2.ap[:, :QB]), r=[PSK[6], T2], w=[T2])
                            P.dve(lambda e: e.scalar_tensor_tensor(out=T1.ap[:, :QB], in0=T2.ap[:, :QB], scalar=LS[:, 2:3], in1=T1.ap[:, :QB],
                                                                   op0=ALU.mult, op1=ALU.add), r=[T1, T2, 'LS'], w=[T1])
                            P.act(lambda e: e.activation(out=T2.ap[:, :QB], in_=T1.ap[:, :QB], func=AF.Square), r=[T1], w=[T2])
                            P.pe(lambda e: e.matmul(ps[0][:, :QB], lhsT=ones_f[:], rhs=T2.ap[:, :QB], start=True, stop=True),
                                 r=[T2, 'ones'], w=[PSK[0]])
                            P.act(lambda e: e.activation(out=T2.ap[:, :QB], in_=ps[0][:, :QB], func=AF.Sqrt, scale=1.0 / 128, bias=EPS),
                                  r=[PSK[0]], w=[T2])
                            P.dve(lambda e: e.reciprocal(out=T2.ap[:, :QB], in_=T2.ap[:, :QB]), r=[T2], w=[T2])
                            P.dve(lambda e: e.tensor_mul(out=T1.ap[:, :QB], in0=T1.ap[:, :QB], in1=T2.ap[:, :QB]), r=[T1, T2], w=[T1])
                            P.act(lambda e: e.activation(out=MIX[:, h, q0:q0 + QB], in_=T1.ap[:, :QB], func=AF.Identity, scale=LS[:, 3:4]),
                                  r=[T1, 'LS'], w=[('MIX', h)])

                        P.dve(lambda e: e.tensor_copy(out=QTb.ap, in_=QT.ap), r=[QT], w=[QTb])
                        P.dve(lambda e: e.memset(QTb.ap[0:64, :], 0.0), r=[QTb], w=[QTb])
                        P.dve(lambda e: e.memset(QT.ap[64:128, :], 0.0), r=[QT, QTb], w=[QT])
                        attn_core([(0, 128, QT), (0, 128, QTb)], KT,
                                  lambda i, kt, hh=hh: (V2v[:, kt, hh * 128:(hh + 1) * 128], [V2]),
                                  A_SCALE, n_seq, L, ctx, ET, post)

            if 'eattn' not in SKIP:
                eattn()

        A_SCALE = 64 ** -0.5
        D_SCALE = 128 ** -0.5

        def odd_mixer(o, l, grp, n_seq, L, ctx):
            win = D["odd_w_in"][o].rearrange("(kc p) n -> p kc n", p=128)
            SPK = 'SP'
            def sp(i):
                return SP[:, i, :]
            are, aim, ldt = ARE[:, o, :], AIM[:, o, :], LDT[:, o, :]
            rd = ['c_a_reT', 'c_a_imT', 'c_ldtT', 'c_h0T', SPK]
            P.act(lambda e: e.activation(out=sp(7), in_=ldt, func=AF.Exp), r=rd, w=[SPK])
            P.dve(lambda e: e.tensor_mul(out=sp(0), in0=are, in1=sp(7)), r=rd, w=[SPK])
            P.act(lambda e: e.activation(out=sp(0), in_=sp(0), func=AF.Exp), r=rd, w=[SPK])
            P.dve(lambda e: e.tensor_mul(out=sp(1), in0=aim, in1=sp(7)), r=rd, w=[SPK])
            P.dve(lambda e: e.tensor_scalar_mul(out=sp(1), in0=sp(1), scalar1=float(1.0 / (2 * math.pi))), r=rd, w=[SPK])
            for (dsti, off) in ((3, 0.0), (2, 0.25)):
                P.dve(lambda e, off=off: e.tensor_scalar_add(out=sp(8), in0=sp(1), scalar1=off), r=rd, w=[SPK])
                P.dve(lambda e: e.tensor_copy(out=SPI[:], in_=sp(8)), r=rd, w=[SPK])
                P.dve(lambda e: e.tensor_copy(out=sp(11), in_=SPI[:]), r=rd, w=[SPK])
                P.dve(lambda e: e.tensor_sub(out=sp(8), in0=sp(8), in1=sp(11)), r=rd, w=[SPK])
                P.act(lambda e, dsti=dsti: e.activation(out=sp(dsti), in_=sp(8), func=AF.Sin, scale=6.283185), r=rd, w=[SPK])
            P.dve(lambda e: e.tensor_mul(out=sp(2), in0=sp(2), in1=sp(0)), r=rd, w=[SPK])
            P.dve(lambda e: e.tensor_mul(out=sp(3), in0=sp(3), in1=sp(0)), r=rd, w=[SPK])
            P.dve(lambda e: e.tensor_mul(out=sp(7), in0=are, in1=are), r=rd, w=[SPK])
            P.dve(lambda e: e.tensor_mul(out=sp(8), in0=aim, in1=aim), r=rd, w=[SPK])
            P.dve(lambda e: e.tensor_add(out=sp(7), in0=sp(7), in1=sp(8)), r=rd, w=[SPK])
            P.dve(lambda e: e.reciprocal(out=sp(7), in_=sp(7)), r=rd, w=[SPK])
            P.dve(lambda e: e.tensor_scalar_add(out=sp(8), in0=sp(2), scalar1=-1.0), r=rd, w=[SPK])
            P.dve(lambda e: e.tensor_mul(out=sp(4), in0=sp(8), in1=are), r=rd, w=[SPK])
            P.dve(lambda e: e.tensor_mul(out=sp(11), in0=sp(3), in1=aim), r=rd, w=[SPK])
            P.dve(lambda e: e.tensor_add(out=sp(4), in0=sp(4), in1=sp(11)), r=rd, w=[SPK])
            P.dve(lambda e: e.tensor_mul(out=sp(4), in0=sp(4), in1=sp(7)), r=rd, w=[SPK])
            P.dve(lambda e: e.tensor_mul(out=sp(5), in0=sp(3), in1=are), r=rd, w=[SPK])
            P.dve(lambda e: e.tensor_mul(out=sp(11), in0=sp(8), in1=aim), r=rd, w=[SPK])
            P.dve(lambda e: e.tensor_sub(out=sp(5), in0=sp(5), in1=sp(11)), r=rd, w=[SPK])
            P.dve(lambda e: e.tensor_mul(out=sp(5), in0=sp(5), in1=sp(7)), r=rd, w=[SPK])
            P.dve(lambda e: e.tensor_scalar_mul(out=sp(6), in0=sp(5), scalar1=-1.0), r=rd, w=[SPK])
            if ctx:
                h0re, h0im = H0[:, o, 0, :], H0[:, o, 1, :]
                P.dve(lambda e: e.tensor_mul(out=sp(9), in0=sp(2), in1=h0re), r=rd, w=[SPK])
                P.dve(lambda e: e.tensor_mul(out=sp(11), in0=sp(3), in1=h0im), r=rd, w=[SPK])
                P.dve(lambda e: e.tensor_sub(out=sp(9), in0=sp(9), in1=sp(11)), r=rd, w=[SPK])
                P.dve(lambda e: e.tensor_mul(out=sp(10), in0=sp(2), in1=h0im), r=rd, w=[SPK])
                P.dve(lambda e: e.tensor_mul(out=sp(11), in0=sp(3), in1=h0re), r=rd, w=[SPK])
                P.dve(lambda e: e.tensor_add(out=sp(10), in0=sp(10), in1=sp(11)), r=rd, w=[SPK])

            def s5():
                U = scr(0, 512, BF16)
                BRE = scr(512, 1024); BIM = scr(1536, 1024); GRE = scr(2560, 1024); GIM = scr(3584, 1024)
                HRE = sub(BRE, BRE.ap[:, 0:512].bitcast(BF16)); HIM = sub(BIM, BIM.ap[:, 0:512].bitcast(BF16))
                BL = scr(4608, 1024, BF16); CL = scr(5632, 1024, BF16)
                BLv = BL.ap.rearrange("p (a c) -> p a c", a=16); CLv = CL.ap.rearrange("p (a c) -> p a c", a=16)
                BZ = scr(6656, 256); CZ = scr(6912, 256)
                BZv = BZ.ap.rearrange("p (a c) -> p a c", a=2); CZv = CZ.ap.rearrange("p (a c) -> p a c", a=2)
                BT1 = scr(7168, 128); BT2 = scr(7424, 128)
                T1 = scr(7680, 512); T2 = scr(8192, 512)
                KF = Buf(SCR[:, 7680:8704], T1.keys + T2.keys)
                FS = scr(8704, 512)
                FSv = FS.ap.rearrange("p (d r j s) -> p d r j s", d=2, r=2, j=32)
                COS = mixf(0, 1024); SIN = mixf(1024, 1024); DU = mixf(2048, 1024)
                YT = GRE
                IT = sub(GIM, GIM.ap.bitcast(I32))
                YG = lambda c: ('MIX', 8 + c)
                TVt = mixf(3072, 1024)
                P.dma('sp', lambda e: e.dma_start(out=TVt.ap, in_=D["tvec"]), w=[TVt])
                segs = []
                for hv in range(2):
                    if n_seq == 1:
                        segs.append((hv * 512, 512, 0, hv))
                    else:
                        for q in range(512 // L):
                            sq_ = hv * (512 // L) + q
                            segs.append((sq_ * L, L, sq_, hv))

                def views(buf_ap, n0, ln, sq_, d):
                    if d == 0:
                        return buf_ap[:, n0:n0 + ln]
                    e0 = (2 * sq_ + 1) * L - 1 - n0
                    return buf_ap[:, e0 - ln + 1:e0 + 1][:, ::-1]

                def tviews(tab_ap, n0, ln, sq_, d):
                    t0 = n0 - sq_ * L
                    if d == 0:
                        return tab_ap[:, t0:t0 + ln]
                    tp0 = L - 1 - t0
                    return tab_ap[:, tp0 - ln + 1:tp0 + 1][:, ::-1]

                for c in range(8):
                    s = wload([(lambda s: wview(s, 16, 256)[:, :, 0:128], win[:, :, c * 128:(c + 1) * 128])])
                    proj_fm(s, 0, [0, 1])
                    for hv in range(2):
                        P.act(lambda e, hv=hv: e.copy(out=U.ap[:, hv_sl(hv)], in_=ps[hv][:]), r=[PSK[hv]], w=[U])
                        P.act(lambda e, hv=hv, c=c: e.activation(out=DU.ap[:, hv_sl(hv)], in_=ps[hv][:], func=AF.Identity,
                                                                 scale=SSMD[:, o, c:c + 1]), r=[PSK[hv], 'c_ssm_dT'], w=[DU])
                    first = True
                    for d in range(2):
                        for j in range(4):
                            jt = 4 * c + j
                            idx = d * 32 + jt
                            ada_steps(NEXT_ADA[0], 1)
                            a = (d * 4 + j) * 2
                            P.dma('sp', lambda e, d=d, jt=jt: e.dma_start(out=BZv, in_=D["ssm_bz"][o, d, jt]), w=[BZ])
                            P.dma('sp', lambda e, d=d, jt=jt: e.dma_start(out=CZv, in_=D["ssm_cz"][o, d, jt]), w=[CZ])
                            kre, kim, kimn = SP[:, 4, idx:idx + 1], SP[:, 5, idx:idx + 1], SP[:, 6, idx:idx + 1]
                            P.dve(lambda e, kre=kre: e.tensor_scalar_mul(out=BT1.ap, in0=BZv[:, 0, :], scalar1=kre), r=[BZ, SPK], w=[BT1])
                            P.dve(lambda e, kimn=kimn: e.scalar_tensor_tensor(out=BT1.ap, in0=BZv[:, 1, :], scalar=kimn, in1=BT1.ap,
                                                                              op0=ALU.mult, op1=ALU.add), r=[BZ, BT1, SPK], w=[BT1])
                            P.dve(lambda e, kre=kre: e.tensor_scalar_mul(out=BT2.ap, in0=BZv[:, 1, :], scalar1=kre), r=[BZ, SPK], w=[BT2])
                            P.dve(lambda e, kim=kim: e.scalar_tensor_tensor(out=BT2.ap, in0=BZv[:, 0, :], scalar=kim, in1=BT2.ap,
                                                                            op0=ALU.mult, op1=ALU.add), r=[BZ, BT2, SPK], w=[BT2])
                            P.pe(lambda e: e.transpose(out=ps[6][:, 0:128], in_=BT1.ap, identity=ident[:]), r=[BT1, 'c_ident'], w=[PSK[6]])
                            P.pe(lambda e: e.transpose(out=ps[7][:, 0:128], in_=BT2.ap, identity=ident[:]), r=[BT2, 'c_ident'], w=[PSK[7]])
                            P.act(lambda e, a=a: e.copy(out=BLv[:, a, :], in_=ps[6][:, 0:128]), r=[PSK[6]], w=[BL])
                            P.act(lambda e, a=a: e.copy(out=BLv[:, a + 1, :], in_=ps[7][:, 0:128]), r=[PSK[7]], w=[BL])
                            P.act(lambda e, a=a: e.copy(out=CLv[:, a, :], in_=CZv[:, 0, :]), r=[CZ], w=[CL])
                            P.act(lambda e, a=a: e.activation(out=CLv[:, a + 1, :], in_=CZv[:, 1, :], func=AF.Identity, scale=-1.0),
                                  r=[CZ], w=[CL])
                            for hv in range(2):
                                P.pe(lambda e, a=a, hv=hv: e.matmul(ps[hv][:], lhsT=BLv[:, a, :], rhs=U.ap[:, hv_sl(hv)], start=True, stop=True),
                                     r=[BL, U], w=[PSK[hv]])
                                P.pe(lambda e, a=a, hv=hv: e.matmul(ps[2 + hv][:], lhsT=BLv[:, a + 1, :], rhs=U.ap[:, hv_sl(hv)], start=True, stop=True),
                                     r=[BL, U], w=[PSK[2 + hv]])
                            thn = SP[:, 1, idx:idx + 1]
                            P.dve(lambda e, thn=thn: e.tensor_scalar_mul(out=YT.ap[:, 0:L], in0=TVt.ap[:, 0:L], scalar1=thn), r=[TVt, SPK], w=[YT])
                            for (tab, off) in ((SIN, 0.0), (COS, 0.25)):
                                if off:
                                    P.dve(lambda e, off=off: e.tensor_scalar_add(out=YT.ap[:, 0:L], in0=YT.ap[:, 0:L], scalar1=off), r=[YT], w=[YT])
                                P.dve(lambda e: e.tensor_copy(out=IT.ap[:, 0:L], in_=YT.ap[:, 0:L]), r=[YT], w=[IT])
                                P.dve(lambda e: e.tensor_copy(out=KF.ap[:, 0:L], in_=IT.ap[:, 0:L]), r=[IT], w=[KF])
                                P.dve(lambda e: e.tensor_sub(out=KF.ap[:, 0:L], in0=YT.ap[:, 0:L], in1=KF.ap[:, 0:L]), r=[YT, KF], w=[KF])
                                P.act(lambda e, tab=tab: e.activation(out=tab.ap[:, 0:L], in_=KF.ap[:, 0:L], func=AF.Sin, scale=6.283185),
                                      r=[KF], w=[tab])
                            for (n0, ln, sq_, hv) in segs:
                                c0 = n0 - hv * 512
                                A_ = ps[hv][:, c0:c0 + ln]; B_ = ps[2 + hv][:, c0:c0 + ln]
                                Cc = tviews(COS.ap, n0, ln, sq_, d); Sn = tviews(SIN.ap, n0, ln, sq_, d)
                                ore = views(BRE.ap, n0, ln, sq_, d); oim = views(BIM.ap, n0, ln, sq_, d)
                                t1 = T1.ap[:, 0:ln]
                                P.dve(lambda e, ore=ore, A_=A_, Cc=Cc: e.tensor_mul(out=ore, in0=A_, in1=Cc), r=[PSK[hv], COS], w=[BRE])
                                P.dve(lambda e, t1=t1, B_=B_, Sn=Sn: e.tensor_mul(out=t1, in0=B_, in1=Sn), r=[PSK[2 + hv], SIN], w=[T1])
                                P.dve(lambda e, ore=ore, t1=t1: e.tensor_add(out=ore, in0=ore, in1=t1), r=[BRE, T1], w=[BRE])
                                P.dve(lambda e, oim=oim, B_=B_, Cc=Cc: e.tensor_mul(out=oim, in0=B_, in1=Cc), r=[PSK[2 + hv], COS], w=[BIM])
                                P.dve(lambda e, t1=t1, A_=A_, Sn=Sn: e.tensor_mul(out=t1, in0=A_, in1=Sn), r=[PSK[hv], SIN], w=[T1])
                                P.dve(lambda e, oim=oim, t1=t1: e.tensor_sub(out=oim, in0=oim, in1=t1), r=[BIM, T1], w=[BIM])
                            if ctx:
                                P.dve(lambda e, idx=idx: e.tensor_add(out=BRE.ap[:, 0:1], in0=BRE.ap[:, 0:1], in1=SP[:, 9, idx:idx + 1]), r=[BRE, SPK], w=[BRE])
                                P.dve(lambda e, idx=idx: e.tensor_add(out=BIM.ap[:, 0:1], in0=BIM.ap[:, 0:1], in1=SP[:, 10, idx:idx + 1]), r=[BIM, SPK], w=[BIM])
                            mag = SP[:, 0, idx:idx + 1]
                            for sq_ in range(n_seq):
                                sl = slice(sq_ * L, (sq_ + 1) * L)
                                P.dve(lambda e, sl=sl, mag=mag: e.tensor_tensor_scan(out=GRE.ap[:, sl], data0=mag.broadcast_to([128, L]),
                                                                                     data1=BRE.ap[:, sl], initial=0.0, op0=ALU.mult, op1=ALU.add),
                                      r=[BRE, SPK], w=[GRE])
                                P.dve(lambda e, sl=sl, mag=mag: e.tensor_tensor_scan(out=GIM.ap[:, sl], data0=mag.broadcast_to([128, L]),
                                                                                     data1=BIM.ap[:, sl], initial=0.0, op0=ALU.mult, op1=ALU.add),
                                      r=[BIM, SPK], w=[GIM])
                            if not ctx:
                                gl_re = GRE.ap[:, L - 1::L]; gl_im = GIM.ap[:, L - 1::L]
                                cl, sl_ = COS.ap[:, L - 1:L], SIN.ap[:, L - 1:L]
                                fre = FSv[:, d, 0, jt, :]; fim = FSv[:, d, 1, jt, :]
                                tq = T1.ap[:, 0:4]
                                P.dve(lambda e, gl_im=gl_im, sl_=sl_, tq=tq: e.tensor_scalar_mul(out=tq, in0=gl_im, scalar1=sl_), r=[GIM, SIN], w=[T1])
                                P.dve(lambda e, gl_re=gl_re, cl=cl, tq=tq, fre=fre: e.scalar_tensor_tensor(
                                    out=fre, in0=gl_re, scalar=cl, in1=tq, op0=ALU.mult, op1=ALU.subtract), r=[GRE, COS, T1], w=[FS])
                                P.dve(lambda e, gl_re=gl_re, sl_=sl_, tq=tq: e.tensor_scalar_mul(out=tq, in0=gl_re, scalar1=sl_), r=[GRE, SIN], w=[T1])
                                P.dve(lambda e, gl_im=gl_im, cl=cl, tq=tq, fim=fim: e.scalar_tensor_tensor(
                                    out=fim, in0=gl_im, scalar=cl, in1=tq, op0=ALU.mult, op1=ALU.add), r=[GIM, COS, T1], w=[FS])
                            for (n0, ln, sq_, hv) in segs:
                                Cc = tviews(COS.ap, n0, ln, sq_, d); Sn = tviews(SIN.ap, n0, ln, sq_, d)
                                gr = views(GRE.ap, n0, ln, sq_, d); gi = views(GIM.ap, n0, ln, sq_, d)
                                t1 = T1.ap[:, 0:ln]; t2 = T2.ap[:, 0:ln]
                                P.dve(lambda e, t1=t1, gr=gr, Cc=Cc: e.tensor_mul(out=t1, in0=gr, in1=Cc), r=[GRE, COS], w=[T1])
                                P.dve(lambda e, t2=t2, gi=gi, Sn=Sn: e.tensor_mul(out=t2, in0=gi, in1=Sn), r=[GIM, SIN], w=[T2])
                                P.dve(lambda e, t1=t1, t2=t2, n0=n0, ln=ln: e.tensor_sub(out=HRE.ap[:, n0:n0 + ln], in0=t1, in1=t2), r=[T1, T2], w=[HRE])
                                P.dve(lambda e, t1=t1, gr=gr, Sn=Sn: e.tensor_mul(out=t1, in0=gr, in1=Sn), r=[GRE, SIN], w=[T1])
                                P.dve(lambda e, t2=t2, gi=gi, Cc=Cc: e.tensor_mul(out=t2, in0=gi, in1=Cc), r=[GIM, COS], w=[T2])
                                P.dve(lambda e, t1=t1, t2=t2, n0=n0, ln=ln: e.tensor_add(out=HIM.ap[:, n0:n0 + ln], in0=t1, in1=t2), r=[T1, T2], w=[HIM])
                            last = (d == 1 and j == 3)
                            for hv in range(2):
                                P.pe(lambda e, a=a, hv=hv, first=first: e.matmul(ps[4 + hv][:], lhsT=CLv[:, a, :], rhs=HRE.ap[:, hv_sl(hv)],
                                                                                start=first, stop=False), r=[CL, HRE], w=[PSK[4 + hv]])
                                P.pe(lambda e, a=a, hv=hv, last=last: e.matmul(ps[4 + hv][:], lhsT=CLv[:, a + 1, :], rhs=HIM.ap[:, hv_sl(hv)],
                                                                              start=False, stop=last), r=[CL, HIM], w=[PSK[4 + hv]])
                            first = False
                    for hv in range(2):
                        P.dve(lambda e, hv=hv: e.tensor_add(out=T1.ap, in0=ps[4 + hv][:], in1=DU.ap[:, hv_sl(hv)]), r=[PSK[4 + hv], DU], w=[T1])
                        P.act(lambda e: e.activation(out=T2.ap, in_=T1.ap, func=AF.Square), r=[T1], w=[T2])
                        P.dve(lambda e: e.tensor_scalar(out=T2.ap, in0=T2.ap, scalar1=0.044715, scalar2=1.0, op0=ALU.mult, op1=ALU.add), r=[T2], w=[T2])
                        P.dve(lambda e: e.tensor_mul(out=T2.ap, in0=T2.ap, in1=T1.ap), r=[T1, T2], w=[T2])
                        P.act(lambda e: e.activation(out=T2.ap, in_=T2.ap, func=AF.Sigmoid, scale=1.5957691216057308), r=[T2], w=[T2])
                        P.dve(lambda e, hv=hv, c=c: e.tensor_mul(out=MIX[:, 8 + c, hv_sl(hv)], in0=T2.ap, in1=T1.ap), r=[T1, T2], w=[YG(c)])
                if not ctx:
                    P.dma('sp', lambda e: e.dma_start(out=D["o_ssm"][o], in_=FSv), r=[FS])
                gv = D["glu_w"][o].rearrange("(kc p) n -> p kc n", p=128)
                for blk in range(2):
                    s = wload([(lambda s: wview(s, 8, 512), gv[:, :, blk * 512:(blk + 1) * 512])])
                    for mi in range(4):
                        m = blk * 4 + mi
                        for hv in range(2):
                            b = nbank(0, 4)
                            for kc in range(8):
                                P.pe(lambda e, s=s, mi=mi, hv=hv, kc=kc, b=b: e.matmul(
                                    ps[b][:], lhsT=wview(s, 8, 512)[:, kc, mi * 128:(mi + 1) * 128], rhs=MIX[:, 8 + kc, hv_sl(hv)],
                                    start=(kc == 0), stop=(kc == 7)), r=wrk(s) + [YG(kc)], w=[PSK[b]])
                            P.act(lambda e, b=b, m=m: e.activation(out=T2.ap, in_=ps[b][:], func=AF.Sigmoid, bias=GLUB[:, o, m:m + 1]),
                                  r=[PSK[b], 'c_glu_bT'], w=[T2])
                            P.dve(lambda e, m=m, hv=hv: e.tensor_mul(out=MIX[:, m, hv_sl(hv)], in0=MIX[:, 8 + m, hv_sl(hv)], in1=T2.ap),
                                  r=[YG(m), T2], w=[('MIX', m)])

            if 's5' not in SKIP:
                s5()
            def gqa():
                KT = scr(0, 768, BF16)
                V1 = scr(768, 768, BF16)
                V1v = V1.ap.rearrange("p (t c) -> p t c", t=12)
                QT2 = [scr(1536, 512, BF16), scr(2048, 512, BF16)]
                ET = [scr(2560 + i * 256, 256, BF16) for i in range(4)]
                T1 = scr(3584, 512); T2 = scr(4096, 512); QBb = scr(4608, 256, BF16)
                STG = [scr(4864, 512), scr(5376, 512)]
                COS = scr(5888, 1024); SIN = scr(6912, 1024)
                QF = scr(7936, 512)
                stg_n = [0]
                if ctx:
                    P.dma('sp', lambda e: e.dma_start(out=COS.ap, in_=D["ropeD"][0]), w=[COS])
                    P.dma('sp', lambda e: e.dma_start(out=SIN.ap, in_=D["ropeD"][1]), w=[SIN])

                def norm_rope(b, hv, gcol, dst_ap, dstk, out_dram=None):
                    P.act(lambda e: e.activation(out=T2.ap, in_=ps[b][:], func=AF.Square), r=[PSK[b]], w=[T2])
                    P.pe(lambda e: e.matmul(ps[6][:], lhsT=ones_f[:], rhs=T2.ap, start=True, stop=True), r=[T2, 'ones'], w=[PSK[6]])
                    P.act(lambda e: e.activation(out=T2.ap, in_=ps[6][:], func=AF.Sqrt, scale=1.0 / 128, bias=EPS), r=[PSK[6]], w=[T2])
                    P.dve(lambda e: e.reciprocal(out=T2.ap, in_=T2.ap), r=[T2], w=[T2])
                    P.dve(lambda e: e.tensor_mul(out=QF.ap, in0=ps[b][:], in1=T2.ap), r=[PSK[b], T2], w=[QF])
                    if ctx:
                        P.act(lambda e: e.activation(out=QF.ap, in_=QF.ap, func=AF.Identity, scale=gcol), r=[QF, 'c_qk_normT'], w=[QF])
                        rope_combine(b, hv, dst_ap, dstk, QF, QBb, T1, T2, COS, SIN, permD, 7)
                    else:
                        P.act(lambda e: e.activation(out=dst_ap, in_=QF.ap, func=AF.Identity, scale=gcol), r=[QF, 'c_qk_normT'], w=dstk)
                        if out_dram is not None:
                            sg = STG[stg_n[0] % 2]; stg_n[0] += 1
                            P.act(lambda e, sg=sg: e.activation(out=sg.ap, in_=QF.ap, func=AF.Identity, scale=gcol), r=[QF, 'c_qk_normT'], w=[sg])
                            P.dma('sp', lambda e, sg=sg: e.dma_start(out=out_dram, in_=sg.ap), r=[sg])

                for g in range(2):
                    s = wload([(lambda s: wview(s, 16, 256)[:, :, 0:128], win[:, :, 2048 + g * 128:2048 + (g + 1) * 128]),
                               (lambda s: wview(s, 16, 256)[:, :, 128:256], win[:, :, 2304 + g * 128:2304 + (g + 1) * 128])])
                    proj_fm(s, 0, [0, 1])
                    for hv in range(2):
                        norm_rope(hv, hv, QKN[:, o, 1:2], KT.ap[:, hv_sl(hv)], [KT],
                                  None if ctx else D["o_dkT"][o, g, :, hv_sl(hv)])
                    for tt in range(8):
                        b = nbank(2, 4)
                        for kc in range(16):
                            P.pe(lambda e, s=s, tt=tt, kc=kc, b=b: e.matmul(
                                ps[b][:, 0:128], lhsT=H[:, kc, tt * 128:(tt + 1) * 128], rhs=wview(s, 16, 256)[:, kc, 128:256],
                                start=(kc == 0), stop=(kc == 15)), r=wrk(s) + [HK(kc)], w=[PSK[b]])
                        P.act(lambda e, tt=tt, b=b: e.copy(out=V1v[:, tt, :], in_=ps[b][:, 0:128]), r=[PSK[b]], w=[V1])
                        if not ctx:
                            sg = STG[stg_n[0] % 2]; stg_n[0] += 1
                            P.act(lambda e, b=b, sg=sg: e.copy(out=sg.ap[:, 0:128], in_=ps[b][:, 0:128]), r=[PSK[b]], w=[sg])
                            P.dma('sp', lambda e, tt=tt, sg=sg, g=g: e.dma_start(
                                out=D["o_dv"][o, tt * 128:(tt + 1) * 128, g * 128:(g + 1) * 128], in_=sg.ap[:, 0:128]), r=[sg])
                    if ctx:
                        P.dma('pool', lambda e, g=g: e.dma_start(out=KT.ap[:, 1024:1536], in_=D["cd_kT"][o, g]), w=[KT])
                        P.dma('pool', lambda e, g=g: e.dma_start(out=V1v[:, 8:12, :],
                                                                 in_=D["cd_v"][o, g].rearrange("(j p) d -> p j d", p=128)), w=[V1])
                    for rp in range(2):
                        h0_ = g * 4 + rp * 2
                        s = wload([(lambda s: wview(s, 16, 256), win[:, :, 1024 + h0_ * 128:1024 + (h0_ + 2) * 128])])
                        for i in range(2):
                            proj_fm(s, i * 128, [0, 1])
                            for hv in range(2):
                                norm_rope(hv, hv, QKN[:, o, 0:1], QT2[i].ap[:, hv_sl(hv)], [QT2[i]])

                        def post(q0, QB, h0_=h0_):
                            for i in range(2):
                                tq = T1 if i == 0 else T2
                                P.dve(lambda e, i=i, tq=tq: e.reciprocal(out=tq.ap[:, :QB], in_=ps[5 + 2 * i][:, :QB]), r=[PSK[5 + 2 * i]], w=[tq])
                                P.dve(lambda e, i=i, tq=tq: e.tensor_mul(out=MIX[:, 8 + h0_ + i, q0:q0 + QB], in0=ps[4 + 2 * i][:, :QB],
                                                                         in1=tq.ap[:, :QB]), r=[PSK[4 + 2 * i], tq], w=[('MIX', 8 + h0_ + i)])

                        attn_core([(0, 128, QT2[0]), (0, 128, QT2[1])], KT, lambda i, kt: (V1v[:, kt, :], [V1]),
                                  D_SCALE, n_seq, L, ctx, ET, post)

            if 'gqa' not in SKIP:
                gqa()

        for grp in range(2):
            if ('grp%d' % grp) in SKIP:
                continue
            cnd = grp
            n_seq, L, ctx = (1, 1024, True) if grp == 0 else (4, 256, False)
            for kc in range(16):
                P.dma('sp', lambda e, kc=kc, grp=grp: e.dma_start(out=X[:, kc, :], in_=D["xg"][grp, :, kc, :]), w=[XK(kc)])
            for l in range(NLAYERS):
                if grp == 0 and l + 1 < 4:
                    NEXT_ADA[0] = ada_blocks(l + 1, (lambda: 4) if l % 2 == 0 else (lambda: 6))
                else:
                    NEXT_ADA[0] = None
                phase_norm(l, 0, cnd)
                if l % 2 == 0:
                    if 'even' not in SKIP:
                        even_mixer(l // 2, l, grp, n_seq, L, ctx)
                    if os.environ.get("MK_DBG") == "mix" and grp == 0 and l == 0:
                        for i_, c_ in enumerate([0, 3, 7, 8, 12, 15]):
                            tb = scr(0, 1024)
                            P.dve(lambda e, c_=c_, tb=tb: e.tensor_copy(out=tb.ap, in_=MIX[:, c_, :]), r=[('MIX', c_)], w=[tb])
                            P.dma('sp', lambda e, i_=i_, tb=tb: e.dma_start(out=D["dbg"][i_], in_=tb.ap), r=[tb])
                    if 'wout' not in SKIP:
                        phase_wout(D["even_w_out"][l // 2], l, cnd)
                else:
                    if 'odd' not in SKIP:
                        odd_mixer(l // 2, l, grp, n_seq, L, ctx)
                    if 'wout' not in SKIP:
                        phase_wout(D["odd_w_out"][l // 2], l, cnd)
                ada_steps(NEXT_ADA[0], 48)
                phase_norm(l, 1, cnd)
                if 'mlp' not in SKIP:
                    phase_mlp(l, cnd)
            phase_norm(0, 0, cnd, final=True, grp=grp)
        P.emit()
    return nc


NLAYERS = int(os.environ.get("MK_NLAYERS", "4"))
SKIP = set(os.environ.get("MK_SKIP", "").split(","))
NCORES = int(os.environ.get("MK_NCORES", "8"))

_CACHE = {}


def _rope_tables(dim, nheads_rep):
    GRID_W = 64
    rows = TT // GRID_W
    row = np.repeat(np.arange(rows, dtype=np.float32), GRID_W)
    col = np.tile(np.arange(GRID_W, dtype=np.float32), rows)
    quarter = dim // 4
    inv_freq = (np.float32(10000.0) ** (-np.arange(quarter, dtype=np.float32) / np.float32(quarter))).astype(np.float32)
    ang_r = row[:, None] * inv_freq[None, :]
    ang_c = col[:, None] * inv_freq[None, :]
    ang = np.concatenate([ang_r, ang_r, ang_c, ang_c], axis=-1).astype(np.float32)
    cos = np.cos(ang).astype(np.float32).T
    sin = np.sin(ang).astype(np.float32).T
    sign = np.ones((dim, 1), np.float32)
    sign[0:quarter] = -1
    sign[2 * quarter:3 * quarter] = -1
    sinS = sin * sign
    perm = np.zeros((dim, dim), np.float32)
    for m in range(dim):
        q = m // quarter
        sig = m + quarter if q % 2 == 0 else m - quarter
        perm[sig, m] = 1
    cos = np.tile(cos, (nheads_rep, 1))
    sinS = np.tile(sinS, (nheads_rep, 1))
    permf = np.zeros((dim * nheads_rep, dim * nheads_rep), np.float32)
    for r in range(nheads_rep):
        permf[r * dim:(r + 1) * dim, r * dim:(r + 1) * dim] = perm
    return np.ascontiguousarray(np.stack([cos, sinS])), permf


def _fm(v):
    lead = v.shape[:-1]
    n = v.shape[-1] // 128
    a = v.reshape(*lead, n, 128)
    return np.ascontiguousarray(np.moveaxis(a, -1, 0))


def kernel(**inp):
    inp = {k: np.asarray(v, dtype=np.float32) for k, v in inp.items()}
    if "nc" not in _CACHE:
        _CACHE["nc"] = build()
    nc = _CACHE["nc"]
    f32 = np.float32
    ropeA, permA = _rope_tables(64, 2)
    ropeD, permD = _rope_tables(128, 1)
    shared = {
        "ada_w": inp["ada_w"], "ada_bT": _fm(inp["ada_b"]),
        "norm_gT": _fm(inp["norm_g"]), "final_normT": _fm(inp["final_norm"]),
        "mlp_w1": inp["mlp_w1"], "mlp_w2": inp["mlp_w2"],
        "even_w_in": inp["even_w_in"], "even_w_out": inp["even_w_out"],
        "odd_w_in": inp["odd_w_in"], "odd_w_out": inp["odd_w_out"], "glu_w": inp["ssm_glu_w"],
        "dlam": np.ascontiguousarray(np.broadcast_to(inp["diff_lambda"][None], (128, 2, 4, 64))),
        "sublnT": np.ascontiguousarray(inp["diff_subln"].T),
        "conv_wT": np.ascontiguousarray(inp["conv_w"].reshape(2, 31, 8, 128).transpose(3, 0, 2, 1)),
        "conv_bT": _fm(inp["conv_b"]), "conv_lnT": _fm(inp["conv_ln"]),
        "ssm_dT": _fm(inp["ssm_d"]), "glu_bT": _fm(inp["ssm_glu_b"]),
        "qk_normT": np.ascontiguousarray(inp["qk_norm"].transpose(2, 0, 1)),
        "ident": np.eye(128, dtype=f32), "ropeA": ropeA, "ropeD": ropeD, "permA": permA, "permD": permD,
        "tvec": np.ascontiguousarray(np.broadcast_to(np.arange(TT, dtype=f32)[None], (128, TT))),
    }

    def st_layout(a):
        o_ = a.reshape(2, 2, 32, 2, 64)
        return np.ascontiguousarray(o_.transpose(3, 4, 0, 1, 2).reshape(128, 2, 64))

    shared["a_reT"] = st_layout(inp["ssm_a_re"])
    shared["a_imT"] = st_layout(inp["ssm_a_im"])
    shared["ldtT"] = st_layout(np.broadcast_to(inp["ssm_log_dt"][..., None], (2, 2, 64, 64)))
    b = inp["ssm_b"]
    c = inp["ssm_c"]
    bz = np.zeros((2, 2, 32, 128, 2, 128), f32)
    cz = np.zeros((2, 2, 32, 128, 2, 128), f32)
    for jt in range(32):
        for gl in range(2):
            g = 2 * jt + gl
            c0 = (g % 8) * 16
            bz[:, :, jt, gl * 64:(gl + 1) * 64, :, c0:c0 + 16] = b[:, :, :, g].transpose(0, 1, 3, 2, 4)
            cz[:, :, jt, gl * 64:(gl + 1) * 64, :, c0:c0 + 16] = c[:, :, :, g].transpose(0, 1, 4, 2, 3)
    shared["ssm_bz"] = bz
    shared["ssm_cz"] = cz

    in_maps = []
    for core in range(NCORES):
        sidx = core % 2
        m = dict(shared)
        xs = inp["x_sample"][sidx]
        xp = inp["x_prompt"][core * 4:(core + 1) * 4].reshape(TT, 2048)
        m["xg"] = np.ascontiguousarray(np.stack([xs.T.reshape(16, 128, TT).transpose(1, 0, 2),
                                                 xp.T.reshape(16, 128, TT).transpose(1, 0, 2)]))
        m["cond"] = np.ascontiguousarray(np.stack([_fm(inp["c"][sidx]), _fm(inp["c_ctx"])], axis=-1))
        m["ca_kT"] = np.ascontiguousarray(inp["cache_a_k"][sidx].transpose(0, 1, 3, 2))
        m["ca_v"] = np.ascontiguousarray(inp["cache_a_v"][sidx])
        m["cd_kT"] = np.ascontiguousarray(inp["cache_d_k"][sidx].transpose(0, 1, 3, 2))
        m["cd_v"] = np.ascontiguousarray(inp["cache_d_v"][sidx])
        h0 = inp["state_c_ssm"][sidx]
        h0 = h0.reshape(2, 2, 2, 32, 2, 64).transpose(4, 5, 0, 2, 1, 3).reshape(128, 2, 2, 64)
        m["h0T"] = np.ascontiguousarray(h0)
        in_maps.append(m)
    res = run_bass_kernel_spmd(nc, in_maps, core_ids=list(range(NCORES)))
    R = res.results
    if os.environ.get("MK_DBG"):
        _CACHE["dbg"] = R[0]["dbg"]
    y_prompt = np.zeros((32, 256, 2048), f32)
    y_sample = np.zeros((2, 1024, 2048), f32)
    new_a_k = np.zeros((32, 2, 8, 256, 128), f32)
    new_a_v = np.zeros((32, 2, 8, 256, 128), f32)
    new_d_k = np.zeros((32, 2, 2, 256, 128), f32)
    new_d_v = np.zeros((32, 2, 2, 256, 128), f32)
    new_c = np.zeros((32, 2, 2, 2, 64, 64), f32)
    for core in range(NCORES):
        r = R[core]
        yg = r["yg"]
        ytm = yg.transpose(0, 3, 2, 1).reshape(2, TT, 2048)
        if core < 2:
            y_sample[core] = ytm[0]
        bs = slice(core * 4, (core + 1) * 4)
        y_prompt[bs] = ytm[1].reshape(4, 256, 2048)
        new_a_k[bs] = r["o_akT"].reshape(2, 8, 128, 4, 256).transpose(3, 0, 1, 4, 2)
        new_a_v[bs] = r["o_av"].reshape(2, 4, 256, 8, 128).transpose(1, 0, 3, 2, 4)
        new_d_k[bs] = r["o_dkT"].reshape(2, 2, 128, 4, 256).transpose(3, 0, 1, 4, 2)
        new_d_v[bs] = r["o_dv"].reshape(2, 4, 256, 2, 128).transpose(1, 0, 3, 2, 4)
        t = r["o_ssm"].reshape(2, 2, 64, 2, 2, 32, 4)
        new_c[bs] = t.transpose(6, 0, 3, 4, 5, 1, 2).reshape(4, 2, 2, 2, 64, 64)
    return (y_prompt, y_sample, new_a_k, new_a_v, new_d_k, new_d_v, new_c)
```
